# Optimizing a Trainium2 kernel written in Bass

```python
import jax, jax.numpy as jnp
from jax import lax
import numpy as np

D_MODEL = 1024
BATCH = 8
SEQ = 2048
DEPTH = 1

HEAD_DIM = 64
N_Q_HEADS = 8
N_KV_HEADS = 2
Q_PER_KV = N_Q_HEADS // N_KV_HEADS
ATTN_WIDTH = N_Q_HEADS * HEAD_DIM
KV_WIDTH = N_KV_HEADS * HEAD_DIM
WINDOW = 128
BLOCK = 128
LRU_WIDTH = D_MODEL - ATTN_WIDTH
LRU_HEADS = 8
LRU_HEAD_DIM = LRU_WIDTH // LRU_HEADS
LRU_CONV_WIDTH = 4
LRU_C = 8.0
MIX_WIDTH = ATTN_WIDTH + LRU_WIDTH
IN_WIDTH = ATTN_WIDTH + 2 * KV_WIDTH + 2 * LRU_WIDTH
D_FF = 2816
FFN_CONV_WIDTH = 3
RMS_EPS = 1e-6

kernel_name = "hymba_swa_sink_rglru_convffn_sandwich"


def rmsnorm(x, g):
    xf = x.astype(jnp.float32)
    y = xf * lax.rsqrt(jnp.mean(xf * xf, axis=-1, keepdims=True) + RMS_EPS)
    return (y * g.astype(jnp.float32)).astype(x.dtype)


def causal_dwconv(x, w, b):
    K = w.shape[0]
    S = x.shape[1]
    xp = jnp.pad(x, ((0, 0), (K - 1, 0), (0, 0)))
    y = xp[:, 0:S] * w[0]
    for k in range(1, K):
        y = y + xp[:, k:k + S] * w[k]
    return y + b


def alibi_slopes(n_heads):
    h = jnp.arange(1, n_heads + 1, dtype=jnp.float32)
    return jnp.exp2(-8.0 * h / n_heads)


def sliding_window_attention(q, k, v, sinks):
    B, S, _ = q.shape
    nb = S // BLOCK
    qb = q.reshape(B, nb, BLOCK, N_KV_HEADS, Q_PER_KV, HEAD_DIM)
    kb = k.reshape(B, nb, BLOCK, N_KV_HEADS, HEAD_DIM)
    vb = v.reshape(B, nb, BLOCK, N_KV_HEADS, HEAD_DIM)

    def with_prev(t):
        prev = jnp.pad(t[:, :-1], ((0, 0), (1, 0), (0, 0), (0, 0), (0, 0)))
        return jnp.concatenate([prev, t], axis=2)

    kk = with_prev(kb)
    vv = with_prev(vb)
    scale = HEAD_DIM ** -0.5
    scores = jnp.einsum('bnqhgd,bnkhd->bnhgqk', qb, kk).astype(jnp.float32) * scale

    qi = jnp.arange(BLOCK)[:, None]
    ki = jnp.arange(2 * BLOCK)[None, :]
    dist = (BLOCK + qi - ki)
    in_window = (dist >= 0) & (dist < WINDOW)
    key_exists = (jnp.arange(nb)[:, None, None] > 0) | (ki >= BLOCK)[None]
    mask = in_window[None] & key_exists

    slopes = alibi_slopes(N_Q_HEADS).reshape(N_KV_HEADS, Q_PER_KV)
    bias = -slopes[:, :, None, None] * dist.astype(jnp.float32)
    logits = scores + bias[None, None]
    logits = jnp.where(mask[None, :, None, None], logits, jnp.finfo(jnp.float32).min)

    s = sinks.astype(jnp.float32).reshape(N_KV_HEADS, Q_PER_KV)
    sink_col = jnp.broadcast_to(s[None, None, :, :, None, None], logits.shape[:-1] + (1,))
    probs = jax.nn.softmax(jnp.concatenate([logits, sink_col], axis=-1), axis=-1)[..., :-1]
    out = jnp.einsum('bnhgqk,bnkhd->bnqhgd', probs.astype(v.dtype), vv)
    return out.reshape(B, S, ATTN_WIDTH)


def rg_lru(x, w_a, b_a, w_x, b_x, lam):
    B, S, _ = x.shape
    xh = x.reshape(B, S, LRU_HEADS, LRU_HEAD_DIM)
    r = jax.nn.sigmoid(jnp.einsum('bshi,hij->bshj', xh, w_a).reshape(B, S, LRU_WIDTH) + b_a)
    i = jax.nn.sigmoid(jnp.einsum('bshi,hij->bshj', xh, w_x).reshape(B, S, LRU_WIDTH) + b_x)
    log_a = -LRU_C * r.astype(jnp.float32) * jax.nn.softplus(-lam.astype(jnp.float32))
    a = jnp.exp(log_a)
    mult = jnp.sqrt(-jnp.expm1(2.0 * log_a))
    u = mult * (i * x).astype(jnp.float32)

    def combine(left, right):
        a_l, b_l = left
        a_r, b_r = right
        return a_l * a_r, a_r * b_l + b_r

    _, h = lax.associative_scan(combine, (a, u), axis=1)
    return h.astype(x.dtype)


def setup_inputs(seed: int = 0) -> dict:
    key = jax.random.key(seed)
    ks = jax.random.split(key, 24)
    f32 = jnp.float32

    def nrm(k, shape, scale):
        return jax.random.normal(k, shape, f32) * scale

    def gain(k, n):
        return 1.0 + 0.05 * jax.random.normal(k, (DEPTH, n), f32)

    x = jax.random.normal(ks[0], (BATCH, SEQ, D_MODEL), f32)
    a0 = jax.random.uniform(ks[10], (DEPTH, LRU_WIDTH), f32, 0.9, 0.999)
    base = a0 ** (1.0 / LRU_C)
    lru_lambda = jnp.log(base) - jnp.log1p(-base)
    return {
        "x": x,
        "norm_mix_pre": gain(ks[1], D_MODEL),
        "w_in": nrm(ks[2], (DEPTH, D_MODEL, IN_WIDTH), D_MODEL ** -0.5),
        "sinks": nrm(ks[3], (DEPTH, N_Q_HEADS), 0.5),
        "lru_conv_w": nrm(ks[4], (DEPTH, LRU_CONV_WIDTH, LRU_WIDTH), LRU_CONV_WIDTH ** -0.5),
        "lru_conv_b": nrm(ks[5], (DEPTH, LRU_WIDTH), 0.01),
        "lru_wa": nrm(ks[6], (DEPTH, LRU_HEADS, LRU_HEAD_DIM, LRU_HEAD_DIM), LRU_HEAD_DIM ** -0.5),
        "lru_ba": nrm(ks[7], (DEPTH, LRU_WIDTH), 0.01),
        "lru_wx": nrm(ks[8], (DEPTH, LRU_HEADS, LRU_HEAD_DIM, LRU_HEAD_DIM), LRU_HEAD_DIM ** -0.5),
        "lru_bx": nrm(ks[9], (DEPTH, LRU_WIDTH), 0.01),
        "lru_lambda": lru_lambda,
        "norm_attn_out": gain(ks[11], ATTN_WIDTH),
        "norm_lru_out": gain(ks[12], LRU_WIDTH),
        "w_out": nrm(ks[13], (DEPTH, MIX_WIDTH, D_MODEL), MIX_WIDTH ** -0.5),
        "norm_mix_post": gain(ks[14], D_MODEL),
        "norm_ffn_pre": gain(ks[15], D_MODEL),
        "w_up": nrm(ks[16], (DEPTH, D_MODEL, 2 * D_FF), D_MODEL ** -0.5),
        "ffn_conv_w": nrm(ks[17], (DEPTH, FFN_CONV_WIDTH, 2 * D_FF), FFN_CONV_WIDTH ** -0.5),
        "ffn_conv_b": nrm(ks[18], (DEPTH, 2 * D_FF), 0.01),
        "w_down": nrm(ks[19], (DEPTH, D_FF, D_MODEL), D_FF ** -0.5),
        "norm_ffn_post": gain(ks[20], D_MODEL),
    }


def reference(x, norm_mix_pre, w_in, sinks, lru_conv_w, lru_conv_b, lru_wa, lru_ba,
              lru_wx, lru_bx, lru_lambda, norm_attn_out, norm_lru_out, w_out,
              norm_mix_post, norm_ffn_pre, w_up, ffn_conv_w, ffn_conv_b, w_down,
              norm_ffn_post):
    splits = [ATTN_WIDTH, ATTN_WIDTH + KV_WIDTH, ATTN_WIDTH + 2 * KV_WIDTH,
              ATTN_WIDTH + 2 * KV_WIDTH + LRU_WIDTH]
    for l in range(DEPTH):
        h = rmsnorm(x, norm_mix_pre[l])
        proj = h @ w_in[l]
        q, k, v, lx, lg = jnp.split(proj, splits, axis=-1)
        attn = sliding_window_attention(q, k, v, sinks[l])
        lx = causal_dwconv(lx, lru_conv_w[l], lru_conv_b[l])
        lru = rg_lru(lx, lru_wa[l], lru_ba[l], lru_wx[l], lru_bx[l], lru_lambda[l])
        lru = lru * jax.nn.gelu(lg, approximate=True)
        merged = jnp.concatenate([rmsnorm(attn, norm_attn_out[l]),
                                  rmsnorm(lru, norm_lru_out[l])], axis=-1)
        x = x + rmsnorm(merged @ w_out[l], norm_mix_post[l])
        f = rmsnorm(x, norm_ffn_pre[l]) @ w_up[l]
        f = causal_dwconv(f, ffn_conv_w[l], ffn_conv_b[l])
        gate, val = jnp.split(f, 2, axis=-1)
        f = (jax.nn.gelu(gate, approximate=True) * val) @ w_down[l]
        x = x + rmsnorm(f, norm_ffn_post[l])
    return x
```

```python
import bisect
import contextlib
import numpy as np
import concourse.bass as bass
import concourse.mybir as mybir
from concourse.bass_utils import run_bass_kernel_spmd

F32 = mybir.dt.float32
BF16 = mybir.dt.bfloat16
AF = mybir.ActivationFunctionType
ALU = mybir.AluOpType

S = 2048
D = 1024
NT = 16
NB = 4
DFF = 2816
NCP = 22
EPS = 1e-6
NPP = 216

ENGS = ("pe", "act", "dve", "pool", "sp")


class Op:
    __slots__ = ("eng", "fn", "reads", "writes", "deps", "idx", "is_dma", "slot", "sig", "val")

    def __init__(self, eng, fn, reads, writes, is_dma, slot):
        self.eng, self.fn, self.reads, self.writes = eng, fn, reads, writes
        self.is_dma, self.slot = is_dma, slot
        self.deps = set()
        self.sig = False
        self.val = None


class Prog:
    def __init__(self, nc):
        self.nc = nc
        self.ops = []
        self.last_writer = {}
        self.readers = {}
        self.pending = {}

    @staticmethod
    def fam(k):
        return k if isinstance(k, str) else k[0]

    def frontier(self, fams):
        fams = set(fams)
        out = set()
        for k, j in self.last_writer.items():
            if self.fam(k) in fams:
                out.add(j)
        for k, js in self.readers.items():
            if self.fam(k) in fams:
                out.update(js)
        return out

    def op(self, eng, fn, reads=(), writes=(), dma=False, slot=None):
        o = Op(eng, fn, tuple(reads), tuple(writes), dma, slot)
        o.idx = len(self.ops)
        lw, rd = self.last_writer, self.readers
        if self.pending:
            for k in o.reads + o.writes:
                pd = self.pending.get(self.fam(k))
                if pd:
                    o.deps.update(pd)
        for k in o.reads:
            j = lw.get(k)
            if j is not None:
                o.deps.add(j)
            if self.fam(k) == "ps":
                for r in rd.get(k, ()):
                    if self.ops[r].eng != eng:
                        o.deps.add(r)
        for k in o.writes:
            j = lw.get(k)
            if j is not None:
                o.deps.add(j)
            for r in rd.get(k, ()):
                o.deps.add(r)
        for k in o.reads:
            rd.setdefault(k, []).append(o.idx)
        for k in o.writes:
            lw[k] = o.idx
            rd[k] = []
        o.deps.discard(o.idx)
        self.ops.append(o)
        return o

    def dma(self, eng, out, in_, reads=(), writes=(), slot=None):
        return self.op(eng, lambda e: e.dma_start(out=out, in_=in_), reads, writes,
                       dma=True, slot=slot)

    def emit(self):
        nc = self.nc
        ops = self.ops
        for o in ops:
            for j in o.deps:
                p = ops[j]
                if p.eng == "pe" and o.eng == "pe" and not p.is_dma:
                    continue
                p.sig = True
        slots = sorted({o.slot for o in ops if o.is_dma}, key=str)
        st = contextlib.ExitStack()
        with st:
            esem = {e: st.enter_context(nc.semaphore("s_" + e)) for e in ENGS}
            ssem = {s: st.enter_context(nc.semaphore("d_%s" % (str(s),))) for s in slots}
            ecnt = {e: 0 for e in ENGS}
            scnt = {s: 0 for s in slots}
            slot_hist = {s: [] for s in slots}
            for o in ops:
                if o.is_dma:
                    scnt[o.slot] += 16
                    o.val = scnt[o.slot]
                    slot_hist[o.slot].append((o.idx, o.val))
                elif o.sig:
                    ecnt[o.eng] += 1
                    o.val = ecnt[o.eng]
            per_eng = {e: [o for o in ops if o.eng == e] for e in ENGS}

            def slot_val_before(s, idx):
                h = slot_hist[s]
                k = bisect.bisect_left(h, (idx, -1))
                return h[k - 1][1] if k > 0 else 0

            block = st.enter_context(nc.Block())

            def make(e):
                def body(eng):
                    waited = {}
                    for o in per_eng[e]:
                        need = {}
                        for j in o.deps:
                            p = ops[j]
                            if p.is_dma:
                                key = ("s", p.slot)
                                v = slot_val_before(p.slot, o.idx)
                            else:
                                if p.eng == "pe" and o.eng == "pe":
                                    continue
                                key = ("e", p.eng)
                                v = p.val
                            if v > need.get(key, 0):
                                need[key] = v
                        for key, v in need.items():
                            if waited.get(key, 0) >= v:
                                continue
                            waited[key] = v
                            sem = ssem[key[1]] if key[0] == "s" else esem[key[1]]
                            eng.wait_ge(sem, v)
                        ins = o.fn(eng)
                        if o.is_dma:
                            ins.then_inc(ssem[o.slot], 16)
                        elif o.sig:
                            ins.then_inc(esem[o.eng], 1)
                    if e == "sp":
                        for s in slots:
                            if waited.get(("s", s), 0) < scnt[s]:
                                eng.wait_ge(ssem[s], scnt[s])
                return body

            block.tensor(make("pe"))
            block.scalar(make("act"))
            block.vector(make("dve"))
            block.gpsimd(make("pool"))
            block.sync(make("sp"))


def build_nc(debug=False):
    nc = bass.Bass("TRN2", target_bir_lowering=False)
    dbg_n = [0]

    def din(name, shape):
        return nc.dram_tensor(name, list(shape), F32, kind="ExternalInput").ap()

    x_d = din("x", [S, D])
    win_d = din("w_in_r", [16, 128, 8, 128])
    wout_d = din("w_out_r", [128, 8, 1024])
    wup_d = din("w_up_r", [NCP, 128, 8, 256])
    wdn_d = din("w_down_r", [128, NCP, 1024])
    pp_d = din("pp", [128, NPP])
    gbc_d = din("gbc", [128, 4, 1024])
    btab_d = din("btab", [128, 2048])
    sink_d = din("sinkbc", [128, 8])
    wa_d = din("wab", [128, 4, 128])
    wx_d = din("wxb", [128, 4, 128])
    id_d = din("ident", [128, 128])
    out_d = nc.dram_tensor("out", [S, D], F32, kind="ExternalOutput").ap()

    P = Prog(nc)

    stacks = {"left": [], "right": []}
    retired = {"left": set(), "right": set()}
    ptr = {"left": 16512, "right": 229376 - 64}

    def push(side, name, shape, dt, fams):
        nbytes = int(np.prod(shape[1:])) * (4 if dt == F32 else 2)
        nbytes = (nbytes + 63) // 64 * 64
        if side == "left":
            off = ptr["left"]
            ptr["left"] += nbytes
        else:
            ptr["right"] -= nbytes
            off = ptr["right"]
        assert ptr["left"] <= ptr["right"], ("SBUF overflow", name, ptr)
        t = nc.alloc_sbuf_tensor_at(name, list(shape), dt, offset=off)
        stacks[side].append((nbytes, tuple(fams)))
        allret = retired["left"] | retired["right"]
        if allret:
            fs = frozenset(allret)
            for f in fams:
                P.pending[f] = fs
        return t

    def pop(side, n):
        for _ in range(n):
            nbytes, fams = stacks[side].pop()
            retired[side] |= P.frontier(fams)
            if side == "left":
                ptr["left"] -= nbytes
            else:
                ptr["right"] += nbytes

    L, R = "left", "right"

    pp = push(L, "pp", [128, NPP], F32, ["pp"])
    g2 = push(L, "g2", [128, 2, 1024], F32, ["g2"])
    ident = push(L, "ident", [128, 128], BF16, ["ident"])
    ones = push(L, "ones", [128, 128], BF16, ["ones"])
    wab = push(L, "wab", [128, 4, 128], BF16, ["wab"])
    wxb = push(L, "wxb", [128, 4, 128], BF16, ["wxb"])
    dp = push(L, "dp", [128, 64], F32, ["dp"])
    junk2 = push(L, "junk", [128, 3, 1024], BF16, ["junk"])
    jcnt = [0]

    def jk(n_=1024):
        i = jcnt[0] % 3
        jcnt[0] += 1
        return junk2[:, i, 0:n_], ("junk", i)
    stats = push(L, "stats", [128, 256], F32, ["st"])
    fprev = push(L, "fprev", [128, 2 * NCP, 2], F32, ["fprev"])
    esink = push(L, "esink", [128, 8], F32, ["esink"])
    psc = nc.psum_tensor("ps", [128, 8, 512], F32)
    ps = psc.__enter__()

    def psb(b, n=1):
        return ps[:, b:b + n, :].rearrange("p b n -> p (b n)")

    def psk(b, n=1):
        return [("ps", b + i) for i in range(n)]

    def act(out, in_, func, reads, writes, scale=1.0, bias=0.0, accum=None):
        kw = dict(out=out, in_=in_, func=func, scale=scale, bias=bias)
        if accum is not None:
            kw["accum_out"] = accum
        return P.op("act", lambda e: e.activation(**kw), reads, writes)

    def tt(eng, out, in0, in1, op, reads, writes):
        return P.op(eng, lambda e: e.tensor_tensor(out=out, in0=in0, in1=in1, op=op), reads, writes)

    def stt(out, in0, scalar, in1, op0, op1, reads, writes):
        return P.op("dve", lambda e: e.scalar_tensor_tensor(out=out, in0=in0, scalar=scalar, in1=in1,
                                                            op0=op0, op1=op1), reads, writes)

    def ts(eng, out, in0, s1, s2, op0, op1, reads, writes):
        return P.op(eng, lambda e: e.tensor_scalar(out=out, in0=in0, scalar1=s1, scalar2=s2,
                                                   op0=op0, op1=op1), reads, writes)

    def cp(eng, out, in_, reads, writes):
        return P.op(eng, lambda e: e.tensor_copy(out=out, in_=in_), reads, writes)

    def mm(out, lhsT, rhs, start, stop, reads, writes):
        return P.op("pe", lambda e: e.matmul(out, lhsT=lhsT, rhs=rhs, start=start, stop=stop),
                    reads, writes)

    def tr(out, in_, reads, writes):
        return P.op("pe", lambda e: e.transpose(out, in_, ident[:]), reads, writes)

    def recip(out, in_, reads, writes):
        return P.op("dve", lambda e: e.reciprocal(out=out, in_=in_), reads, writes)

    def dump(name, t, keys):
        if not debug:
            return
        d = nc.dram_tensor("dbg_" + name, list(t.shape), t.dtype, kind="ExternalOutput").ap()
        dbg_n[0] += 1
        P.dma("sp", d, t, reads=keys, slot=("dbg", dbg_n[0]))

    P.dma("sp", pp[:], pp_d[:, :], writes=["pp"], slot="c0")
    P.dma("sp", g2[:, 0, :], gbc_d[:, 0, :], writes=[("g2", 0)], slot=("g2", 0))
    P.dma("sp", g2[:, 1, :], gbc_d[:, 1, :], writes=[("g2", 1)], slot=("g2", 1))
    P.dma("pool", ident[:], id_d[:, :], writes=["ident"], slot="c2")
    P.dma("pool", wab[:], wa_d[:, :, :], writes=["wab"], slot="c3")
    P.dma("pool", wxb[:], wx_d[:, :, :], writes=["wxb"], slot="c4")
    P.dma("sp", esink[:], sink_d[:, :], writes=["esink"], slot="c6")
    P.op("dve", lambda e: e.memset(ones[:], 1.0), writes=["ones"])

    def cw(c, k):
        return pp[:, c * 4 + k:c * 4 + k + 1]
    PP_CB, PP_BA, PP_BX, PP_LAM, PP_FW, PP_FB, PP_GM = 16, 20, 24, 28, 32, 164, 208
    DP_CV, DP_CH, DP_HBA, DP_HBX = 0, 4, 8, 12

    lam = pp[:, PP_LAM:PP_LAM + 4]
    t_ax, t_e, t_l, t_s, t_m, t_mx, t_nl, t_c = (dp[:, 16 + 4 * i:20 + 4 * i] for i in range(8))
    P.op("dve", lambda e: e.memset(t_c, 0.1), writes=["d_c"])
    ts("dve", t_nl, lam, -1.0, None, ALU.mult, ALU.bypass, ["pp"], ["d_nl"])
    tt("dve", t_ax, lam, t_nl, ALU.max, ["pp", "d_nl"], ["d_ax"])
    act(t_e, t_ax, AF.Exp, ["d_ax"], ["d_e"], scale=-1.0)
    act(t_l, t_e, AF.Ln, ["d_e"], ["d_l"], bias=1.0)
    act(esink[:], esink[:], AF.Exp, ["esink"], ["esink"])
    ts("dve", t_s, t_e, -1.0 / 3.0, 0.5, ALU.mult, ALU.add, ["d_e"], ["d_s"])
    tt("dve", t_s, t_s, t_e, ALU.mult, ["d_s", "d_e"], ["d_s"])
    ts("dve", t_s, t_s, -1.0, 1.0, ALU.mult, ALU.add, ["d_s"], ["d_s"])
    tt("dve", t_s, t_s, t_e, ALU.mult, ["d_s", "d_e"], ["d_s"])
    tt("dve", t_m, t_e, t_c, ALU.is_lt, ["d_e", "d_c"], ["d_m"])
    tt("dve", t_s, t_s, t_l, ALU.subtract, ["d_s", "d_l"], ["d_s"])
    tt("dve", t_s, t_s, t_m, ALU.mult, ["d_s", "d_m"], ["d_s"])
    tt("dve", t_l, t_l, t_s, ALU.add, ["d_s", "d_l"], ["d_l"])
    ts("dve", t_mx, t_nl, 0.0, None, ALU.max, ALU.bypass, ["d_nl"], ["d_mx"])
    tt("dve", t_l, t_l, t_mx, ALU.add, ["d_l", "d_mx"], ["d_l"])
    ts("dve", dp[:, DP_CV:DP_CV + 4], t_l, -8.0, None, ALU.mult, ALU.bypass, ["d_l"], ["d_cv"])
    ts("dve", dp[:, DP_CH:DP_CH + 4], t_l, -4.0, None, ALU.mult, ALU.bypass, ["d_l"], ["d_ch"])
    ts("dve", dp[:, DP_HBA:DP_HBA + 4], pp[:, PP_BA:PP_BA + 4], 0.5, None, ALU.mult, ALU.bypass, ["pp"], ["d_hba"])
    ts("dve", dp[:, DP_HBX:DP_HBX + 4], pp[:, PP_BX:PP_BX + 4], 0.5, None, ALU.mult, ALU.bypass, ["pp"], ["d_hbx"])
    DPK = ["d_cv", "d_ch", "d_hba", "d_hbx"]

    ST_A = 0
    ST2 = 48
    ST3 = 144
    ST_N = 192

    def sc(base, k, t):
        return stats[:, base + 16 * k + t:base + 16 * k + t + 1]

    def run_skewed(stages, n):
        ns = len(stages)
        for step in range(n + ns - 1):
            for k, stg in enumerate(stages):
                t = step - k
                if 0 <= t < n:
                    stg(t)
                    yield

    def drain(gen):
        for _ in gen:
            pass

    lruT = push(R, "lruT", [128, 4, S], BF16, ["lruT", "lruT0"])
    lxp = push(L, "lxp", [128, 4, S + 4], BF16, ["lx", "lxpad"])
    glT = push(L, "glT", [128, 4, S], BF16, ["gl"])
    qT = push(L, "qT", [128, 4, S], BF16, ["qT"])
    kT = push(L, "kT", [128, 2, S], BF16, ["kT"])
    vaug = push(L, "vaug", [128, NT, 2, 66], BF16, ["v", "vone"])
    dset = []
    sqb = None

    def D_ALLOC():
        nonlocal sqb
        for i in range(2):
            dset.append(dict(
                xc=push(R, "xc%d" % i, [128, S], F32, ["xc"]),
                xcb=push(R, "xcb%d" % i, [128, S // 2], BF16, ["xcb"]),
                tha=push(R, "tha%d" % i, [128, S], F32, ["tha"]),
                thx=push(R, "thx%d" % i, [128, S], F32, ["thx"]),
                a2=push(R, "a2%d" % i, [128, S], F32, ["a2"])))
        sqb = [push(R, "sqb%d" % i, [128, 512], BF16, ["sqb"]) for i in range(2)]
    N_D = 10 + 2
    hT = push(L, "hT", [128, 8, S], BF16, ["hT"])
    wc = [push(L, "wc%d" % i, [128, 8, 128], BF16, ["wc"]) for i in range(4)]
    wv = push(L, "wv", [128, 8, 128], BF16, ["wv"])
    N_B = 1 + 4 + 1
    NXT, NHB = 8, 4
    xt = [push(L, "xt%d" % i, [128, D], F32, ["xt"]) for i in range(NXT)]
    hb = [push(L, "hb%d" % i, [128, D], BF16, ["hb"]) for i in range(NHB)]
    N_A = NXT + NHB

    P.op("pool", lambda e: e.memset(lxp[:, :, 0:4], 0.0), writes=[("lxpad",)])
    P.op("pool", lambda e: e.memset(vaug[:, :, :, 64:66], 1.0), writes=[("vone",)])

    def a1(t):
        xb = xt[t % NXT]
        P.dma("sp", xb[:], x_d[t * 128:(t + 1) * 128, :], writes=[("xt", t % NXT)], slot=("xt", t % NXT))
        jt, jkey = jk()
        act(jt, xb[:], AF.Square, [("xt", t % NXT)], [("st", "ssA", t), jkey], accum=sc(ST_A, 0, t))

    def a2_(t):
        xb, hbb = xt[t % NXT], hb[t % NHB]
        act(sc(ST_A, 1, t), sc(ST_A, 0, t), AF.Sqrt, [("st", "ssA", t)], [("st", "sdA", t)], scale=1.0 / D, bias=EPS)
        recip(sc(ST_A, 2, t), sc(ST_A, 1, t), [("st", "sdA", t)], [("st", "rsA", t)])
        stt(hbb[:], xb[:], sc(ST_A, 2, t), g2[:, 0, :], ALU.mult, ALU.mult,
            [("xt", t % NXT), ("st", "rsA", t), ("g2", 0)], [("hb", t % NHB)])

    def a3(t):
        hbb = hb[t % NHB]
        bank = t % 2
        pv = psb(bank).bitcast(BF16)
        for c in range(8):
            tr(pv[:, c * 128:(c + 1) * 128], hbb[:, c * 128:(c + 1) * 128],
               [("hb", t % NHB), "ident"], psk(bank))
        cp("dve", hT[:, :, t * 128:(t + 1) * 128], pv.rearrange("p (c n) -> p c n", c=8), psk(bank), [("hT", t)])

    def hTk(tb):
        return [("hT", 4 * tb + j) for j in range(4)]

    bank_rr = [2, 3, 4, 5]
    bank_rr6 = [2, 3, 4, 5, 0, 1]
    use6 = [False]
    nbc = [0]
    evc = [0]

    def b_unit(w, wkey, fc, tb):
        b = (bank_rr6[nbc[0] % 6] if use6[0] else bank_rr[nbc[0] % 4]); nbc[0] += 1
        o = psb(b)
        sl = slice(tb * 512, (tb + 1) * 512)
        for c in range(8):
            mm(o, w[:, c, :], hT[:, c, sl], c == 0, c == 7, hTk(tb) + [wkey], psk(b))
        if fc < 4:
            act(qT[:, fc, sl], o, AF.Copy, psk(b), [("qT", fc, tb)], scale=0.125)
        elif fc < 6:
            cp("dve", kT[:, fc - 4, sl], o, psk(b), [("kT", fc - 4, tb)])
        elif fc < 12:
            c4 = fc - 8
            cp("dve", lxp[:, c4, 4 + tb * 512:4 + (tb + 1) * 512], o, psk(b), [("lx", c4, tb)])
        else:
            c4 = fc - 12
            act(glT[:, c4, sl], o, AF.Gelu_apprx_tanh, psk(b), [("gl", c4, tb)])

    def v_unit(t0):
        b = (bank_rr6[nbc[0] % 6] if use6[0] else bank_rr[nbc[0] % 4]); nbc[0] += 1
        for j in range(4):
            t = t0 + j
            o = psb(b)[:, j * 128:(j + 1) * 128]
            for c in range(8):
                mm(o, hT[:, c, t * 128:(t + 1) * 128], wv[:, c, :], c == 0, c == 7, [("hT", t), "wv"], psk(b))
        dst = vaug[:, t0:t0 + 4, :, 0:64]
        src_ = psb(b).rearrange("p (t h m) -> p t h m", t=4, h=2)
        keys = [("v", t0 + j) for j in range(4)]
        if (t0 // 4) % 2:
            cp("dve", dst, src_, psk(b), keys)
        else:
            act(dst, src_, AF.Copy, psk(b), keys)

    order = [8, 9, 10, 11, 12, 13, 14, 15, 4, 5, 0, 1, 2, 3]
    P.dma("pool", wv[:], win_d[6], writes=["wv"], slot="wv")
    P.dma("pool", wc[0][:], win_d[order[0]], writes=[("wc", 0)], slot=("wc", 0))

    def b_gen():
        for n, fc in enumerate(order):
            w = wc[n % 4]
            if n > 0:
                P.dma("pool", w[:], win_d[fc], writes=[("wc", n % 4)], slot=("wc", n % 4))
            for tb in range(NB):
                b_unit(w, ("wc", n % 4), fc, tb)
                yield (n, tb)
            if n == 9:
                for t in range(0, NT, 4):
                    v_unit(t)
                    yield (n, 0)

    etab = ex = PT = dsm = atok = None

    def C_ALLOC():
        nonlocal etab, ex, PT, dsm, atok
        etab = push(L, "etab", [128, 2, 1024], BF16, ["etab"])
        ex = []
        PT = [push(L, "PT%d" % i, [128, 1024], BF16, ["PT"]) for i in range(2)]
        dsm = push(L, "dsm", [128, 2, 4], F32, ["dsm"])
        atok = push(L, "atok", [128, NT, 512], BF16, ["atok"])
        P.dma("pool", etab[:].rearrange("p h n -> p (h n)"), btab_d[:, :], writes=["etab"], slot="c5")
    N_C = 1 + 2 + 1 + 1

    def S_step(s):
        i, h = s // 2, s % 2
        base = 2 * (s % 2)
        tb = i // 4
        etv = etab[:, h, :].rearrange("p (r n) -> p r n", r=2)
        for rt in range(2):
            b = base + rt
            o = psb(b).rearrange("p (s q) -> p s q", s=4)
            rows = slice(rt * 64, (rt + 1) * 64)
            mm(psb(b), ident[:], etv[:, rt, :], True, False, ["ident", "etab"], psk(b))
            for j in range(2):
                qr = qT[rows, 2 * h + j, i * 128:(i + 1) * 128]
                last = (j == 1)
                mm(o[:, j, :], kT[rows, h, i * 128:(i + 1) * 128], qr, False, last and i == 0,
                   [("kT", h, tb), ("qT", 2 * h + j, tb)], psk(b))
                if i > 0:
                    mm(o[:, 2 + j, :], kT[rows, h, (i - 1) * 128:i * 128], qr, False, last,
                       [("kT", h, (i - 1) // 4), ("qT", 2 * h + j, tb)], psk(b))

    def view4(a):
        return a.rearrange("p (r s q) -> p r s q", r=2, s=4)

    def E_step(s):
        base = 2 * (s % 2)
        act(PT[s % 2][:], psb(base, 2), AF.Exp, psk(base, 2), [("PT", s % 2)])

    def PV_step(s):
        i, h = s // 2, s % 2
        ptv = view4(PT[s % 2][:])
        ob = 4 + s % 2
        o4 = psb(ob).rearrange("p (g n) -> p g n", g=4)
        for g in range(4):
            rt, j = g % 2, g // 2
            mm(o4[:, g, 0:65], ptv[:, rt, j, :], vaug[:, i, h, 0:65], True, i == 0,
               [("PT", s % 2), ("v", i), ("vone",)], psk(ob))
            if i > 0:
                mm(o4[:, g, 0:65], ptv[:, rt, 2 + j, :], vaug[:, i - 1, h, 0:65], False, True,
                   [("PT", s % 2), ("v", i - 1), ("vone",)], psk(ob))
        d = dsm[:, s % 2, :]
        tt("dve", d, o4[:, :, 64], esink[:, 4 * h:4 * h + 4], ALU.add, psk(ob) + ["esink"], [("dsm", s % 2)])
        recip(d, d, [("dsm", s % 2)], [("dsm", s % 2)])
        for g in range(4):
            hq = 4 * h + g
            ts("dve", atok[:, i, hq * 64:(hq + 1) * 64], o4[:, g, 0:64], dsm[:, s % 2, g:g + 1], None,
               ALU.mult, ALU.bypass, psk(ob) + [("dsm", s % 2)], [("atok", i, hq)])

    def c_gen():
        NS = 2 * NT
        S_step(0)
        for s in range(NS):
            if s + 1 < NS:
                S_step(s + 1)
            E_step(s)
            yield
            PV_step(s)
            if s % 2 == 1:
                t_ = s // 2
                jt, jkey = jk(512)
                act(jt, atok[:, t_, :], AF.Square, [("atok", t_, hq) for hq in range(8)],
                    [("st", "ssN", t_), jkey], accum=sc(ST_N, 0, t_))
            yield

    def chunk_gen(c, si):
        H2 = S // 2
        B_ = dset[si]
        xc, xcb, tha, thx, a2 = B_["xc"], B_["xcb"], B_["tha"], B_["thx"], B_["a2"]
        hs = [slice(0, H2), slice(H2, S)]
        LX = [("lx", c, tb) for tb in range(NB)] + [("lxpad",)]
        for hf in range(2):
            ts("dve", xc[:, hs[hf]], lxp[:, c, 4 + hf * H2:4 + (hf + 1) * H2], cw(c, 3),
               pp[:, PP_CB + c:PP_CB + c + 1], ALU.mult, ALU.add, LX + ["pp"], [("xc", si, hf)])
            yield
        for k in (2, 1, 0):
            for hf in range(2):
                stt(xc[:, hs[hf]], lxp[:, c, 1 + k + hf * H2:1 + k + (hf + 1) * H2], cw(c, k), xc[:, hs[hf]],
                    ALU.mult, ALU.add, LX + ["pp", ("xc", si, hf)], [("xc", si, hf)])
                yield
        for tb in range(NB):
            sl = slice(tb * 512, (tb + 1) * 512)
            if tb % 2 == 0:
                cp("dve", xcb[:], xc[:, hs[tb // 2]], [("xc", si, tb // 2)], [("xcb", si)])
                yield
            sl2 = slice((tb % 2) * 512, (tb % 2 + 1) * 512)
            mm(psb(6), wab[:, c, :], xcb[:, sl2], True, True, [("xcb", si), "wab"], psk(6))
            mm(psb(7), wxb[:, c, :], xcb[:, sl2], True, True, [("xcb", si), "wxb"], psk(7))
            act(tha[:, sl], psb(6), AF.Tanh, psk(6) + DPK, [("tha", si, tb)], scale=0.5,
                bias=dp[:, DP_HBA + c:DP_HBA + c + 1])
            act(thx[:, sl], psb(7), AF.Tanh, psk(7) + DPK, [("thx", si, tb)], scale=0.5,
                bias=dp[:, DP_HBX + c:DP_HBX + c + 1])
            yield
        for hf in range(2):
            hk = [("tha", si, 2 * hf), ("tha", si, 2 * hf + 1)]
            act(a2[:, hs[hf]], tha[:, hs[hf]], AF.Exp, hk + DPK, [("a2", si, hf)], scale=dp[:, DP_CV + c:DP_CV + c + 1],
                bias=dp[:, DP_CV + c:DP_CV + c + 1])
            act(tha[:, hs[hf]], tha[:, hs[hf]], AF.Exp, hk + DPK, hk, scale=dp[:, DP_CH + c:DP_CH + c + 1],
                bias=dp[:, DP_CH + c:DP_CH + c + 1])
            yield
            xk = [("thx", si, 2 * hf), ("thx", si, 2 * hf + 1)]
            stt(thx[:, hs[hf]], thx[:, hs[hf]], 1.0, xc[:, hs[hf]], ALU.add, ALU.mult, xk + [("xc", si, hf)], xk)
            yield
        A2 = [("a2", si, 0), ("a2", si, 1)]
        act(a2[:], a2[:], AF.Sqrt, A2, A2, scale=-1.0, bias=1.0)
        yield
        for hf in range(2):
            xk = [("thx", si, 2 * hf), ("thx", si, 2 * hf + 1)]
            stt(thx[:, hs[hf]], thx[:, hs[hf]], 0.5, a2[:, hs[hf]], ALU.mult, ALU.mult, xk + [("a2", si, hf)], xk)
            yield
        for hf in range(2):
            hk = [("tha", si, 2 * hf), ("tha", si, 2 * hf + 1)]
            xk = [("thx", si, 2 * hf), ("thx", si, 2 * hf + 1)]
            init = 0.0 if hf == 0 else a2[:, H2 - 1:H2]
            P.op("dve", lambda e, hf=hf, init=init: e.tensor_tensor_scan(
                out=a2[:, hs[hf]], data0=tha[:, hs[hf]], data1=thx[:, hs[hf]], initial=init,
                op0=ALU.mult, op1=ALU.add), hk + xk + ([("a2", si, 0)] if hf else []), [("a2", si, hf)])
            yield
        for hf in range(2):
            tt("dve", lruT[:, c, hs[hf]], a2[:, hs[hf]], glT[:, c, hs[hf]], ALU.mult,
               [("a2", si, hf)] + [("gl", c, tb) for tb in range(NB)], [("lruT", c)] if hf else [("lruT0", c)])
            yield

    def d_gen():
        def chain2(c0, c1, si):
            for g_ in (chunk_gen(c0, si), chunk_gen(c1, si)):
                for _ in g_:
                    yield
        gens = [chain2(0, 2, 0), chain2(1, 3, 1)]
        alive = [True, True]
        for _ in range(12):
            next(gens[0])
            yield
        while alive[0] or alive[1]:
            for gi in range(2):
                if alive[gi]:
                    try:
                        next(gens[gi])
                        yield
                    except StopIteration:
                        alive[gi] = False
        xc = dset[0]["xc"]
        rbc = xc
        n = 0
        for tb in range(NB):
            sl = slice(tb * 512, (tb + 1) * 512)
            bank = 6 + tb % 2
            for c in range(4):
                sb_ = sqb[n % 2]
                tt("pool", sb_[:], lruT[:, c, sl], lruT[:, c, sl], ALU.mult, [("lruT", c), ("lruT0", c)], [("sqb", n % 2)])
                mm(psb(bank), ones[:], sb_[:], c == 0, c == 3, [("sqb", n % 2), "ones"], psk(bank))
                n += 1
                yield
            act(rbc[:, sl], psb(bank), AF.Ln, psk(bank), [("xc", 0, tb // 2)], scale=1.0 / 512, bias=EPS)
            yield
        act(rbc[:], rbc[:], AF.Exp, [("xc", 0, 0), ("xc", 0, 1)], [("xc", 0, 0), ("xc", 0, 1)], scale=-0.5)
        yield
        for c in range(4):
            tt("dve", lruT[:, c, :], lruT[:, c, :], rbc[:], ALU.mult,
               [("xc", 0, 0), ("xc", 0, 1), ("lruT", c), ("lruT0", c)], [("lruT", c), ("lruT0", c)])
            yield

    def step_gen(g, n=1):
        for _ in range(n):
            try:
                next(g)
            except StopIteration:
                return False
        return True

    gb = b_gen()
    gd = d_gen()
    ns = 3
    for stepi in range(NT + ns - 1):
        for k, stg in enumerate([a1, a2_, a3]):
            t = stepi - k
            if 0 <= t < NT:
                stg(t)
                if k == 2 and t % 4 == 3:
                    next(gb)
    pop(L, N_A)
    D_ALLOC()
    use6[0] = True
    d_alive = True
    vcnt = 0
    for (n, tb) in gb:
        if n >= 8 and d_alive:
            if tb == 4:
                vcnt += 1
                if vcnt % 2 == 0:
                    d_alive = step_gen(gd, 1)
            else:
                d_alive = step_gen(gd, 2)
    P.dma("sp", g2[:, 0, :], gbc_d[:, 2, :], writes=[("g2", 0)], slot=("g2", 0))
    dump("hT", hT[:], [("hT", t) for t in range(NT)])
    dump("qT", qT[:], [("qT", a, b) for a in range(4) for b in range(NB)])
    dump("kT", kT[:], [("kT", a, b) for a in range(2) for b in range(NB)])
    dump("lxp", lxp[:], [("lx", a, b) for a in range(4) for b in range(NB)] + [("lxpad",)])
    dump("glT", glT[:], [("gl", a, b) for a in range(4) for b in range(NB)])
    pop(L, N_B)
    C_ALLOC()
    gc = c_gen()
    c_alive = True
    while c_alive or d_alive:
        if c_alive:
            c_alive = step_gen(gc)
        if d_alive:
            d_alive = step_gen(gd, 2)
    ATOK = [("atok", i, hq) for i in range(NT) for hq in range(8)]
    dump("lruN", lruT[:], [("lruT", c) for c in range(4)])
    dump("xc3", dset[1]["xc"][:], [("xc", 1, 0), ("xc", 1, 1)])
    dump("tha3", dset[1]["tha"][:], [("tha", 1, tb) for tb in range(NB)])
    dump("u3", dset[1]["thx"][:], [("thx", 1, tb) for tb in range(NB)])
    dump("h3", dset[1]["a2"][:], [("a2", 1, 0), ("a2", 1, 1)])
    dump("atok", atok[:], ATOK)
    pop(R, N_D)

    attnT = push(R, "attnT", [128, 4, S], BF16, ["attnT"])
    wo32 = [push(R, "wo32_%d" % i, [128, 2, 1024], F32, ["wo32"]) for i in range(2)]
    wob = push(R, "wob", [128, 8, 1024], BF16, ["wob"])
    for i in range(4):
        P.dma("sp", wo32[i % 2][:], wout_d[:, 2 * i:2 * i + 2, :], writes=[("wo32", i % 2)], slot=("wo32", i % 2))
        for cc in range(2):
            c = 2 * i + cc
            ts("pool", wob[:, c, :], wo32[i % 2][:, cc, :], pp[:, PP_GM + c:PP_GM + c + 1], 1.0, ALU.mult, ALU.mult,
               [("wo32", i % 2), "pp"], [("wob", c)])
    anb = [push(L, "anb%d" % i, [128, 512], BF16, ["anb"]) for i in range(3)]
    SSN = [("st", "ssN", t) for t in range(NT)]
    act(stats[:, ST_N + 16:ST_N + 32], stats[:, ST_N:ST_N + 16], AF.Sqrt, SSN, [("st", "sdN")], scale=1.0 / 512, bias=EPS)
    recip(stats[:, ST_N + 32:ST_N + 48], stats[:, ST_N + 16:ST_N + 32], [("st", "sdN")], [("st", "rsN")])

    def n1(t):
        ts("dve", anb[t % 3][:], atok[:, t, :], sc(ST_N, 2, t), None, ALU.mult, ALU.bypass,
           [("atok", t, hq) for hq in range(8)] + [("st", "rsN")], [("anb", t % 3)])

    def n2(t):
        bank = t % 2
        pv = psb(bank).bitcast(BF16)
        for c in range(4):
            tr(pv[:, c * 128:(c + 1) * 128], anb[t % 3][:, c * 128:(c + 1) * 128], [("anb", t % 3), "ident"], psk(bank))
        src = pv[:, 0:512].rearrange("p (c n) -> p c n", c=4)
        if t % 2:
            cp("dve", attnT[:, :, t * 128:(t + 1) * 128], src, psk(bank), [("attnT", t)])
        else:
            act(attnT[:, :, t * 128:(t + 1) * 128], src, AF.Copy, psk(bank), [("attnT", t)])

    drain(run_skewed([n1, n2], NT))
    dump("attnN", attnT[:], [("attnT", t) for t in range(NT)])
    pop(L, 3)
    pop(L, N_C)
    pop(L, 5)

    h2T = push(L, "h2T", [128, 8, S], BF16, ["h2T"])
    wdn = push(L, "wdn", [128, NCP, 1024], BF16, ["wdn"])
    wu = [push(L, "wu%d" % i, [128, 8, 256], BF16, ["wu"]) for i in range(3)]

    def wu_load(s_):
        P.dma("pool", wu[s_ % 3][:], wup_d[s_ % NCP], writes=[("wu", s_ % 3)], slot=("wu", s_ % 3))
    NXE, NYE, NHE = 5, 2, 3
    xe = [push(R, "xe%d" % i, [128, D], F32, ["xe"]) for i in range(NXE)]
    ye = [push(R, "ye%d" % i, [128, D], F32, ["ye"]) for i in range(NYE)]
    h2b = [push(R, "h2b%d" % i, [128, D], BF16, ["h2b"]) for i in range(NHE)]
    N_E = NXE + NYE + NHE

    for pc in range(4):
        k0, k1 = [0, 6, 12, 17][pc], [6, 12, 17, 22][pc]
        P.dma("pool", wdn[:, k0:k1, :], wdn_d[:, k0:k1, :], writes=[("wdn", pc)], slot=("wdn", pc))
    WDN = [("wdn", pc) for pc in range(4)]
    wu_load(0)
    wu_load(1)

    def e1(t):
        yb = 2 * (t % 3)
        tsl = slice(t * 128, (t + 1) * 128)
        if t == 0:
            for t2 in range(2):
                P.dma("sp", xe[t2 % NXE][:], x_d[t2 * 128:(t2 + 1) * 128, :], writes=[("xe", t2 % NXE)],
                      slot=("xe", t2 % NXE))
        if t + 2 < NT:
            t2 = t + 2
            P.dma("sp", xe[t2 % NXE][:], x_d[t2 * 128:(t2 + 1) * 128, :], writes=[("xe", t2 % NXE)],
                  slot=("xe", t2 % NXE))
        for dh in range(2):
            o = psb(yb + dh)
            for c in range(8):
                src = attnT if c < 4 else lruT
                keys = [("attnT", t)] if c < 4 else [("lruT", c - 4)]
                mm(o, src[:, c % 4, tsl], wob[:, c, dh * 512:(dh + 1) * 512], c == 0, c == 7,
                   keys + [("wob", c)], psk(yb + dh))
        jt, jkey = jk()
        act(jt, psb(yb, 2), AF.Square, psk(yb, 2), [("st", "e_ss", t), jkey], accum=sc(ST2, 0, t))

    def e2(t):
        yb = 2 * (t % 3)
        tsl = slice(t * 128, (t + 1) * 128)
        xb, yy = xe[t % NXE], ye[t % NYE]
        XK = ("xe", t % NXE)
        act(sc(ST2, 1, t), sc(ST2, 0, t), AF.Sqrt, [("st", "e_ss", t)], [("st", "e_sd", t)], scale=1.0 / D, bias=EPS)
        recip(sc(ST2, 2, t), sc(ST2, 1, t), [("st", "e_sd", t)], [("st", "e_rs", t)])
        stt(yy[:], psb(yb, 2), sc(ST2, 2, t), g2[:, 1, :], ALU.mult, ALU.mult,
            psk(yb, 2) + [("st", "e_rs", t), ("g2", 1)], [("ye", t % NYE)])
        tt("dve", xb[:], xb[:], yy[:], ALU.add, [XK, ("ye", t % NYE)], [XK])
        P.dma("sp", out_d[tsl, :], xb[:], reads=[XK], writes=[("out", t)], slot=("x1o", t % NXE))
        if debug:
            if t == 0:
                dbgx1.append(nc.dram_tensor("dbg_x1", [S, D], F32, kind="ExternalOutput").ap())
            P.dma("sp", dbgx1[0][tsl, :], xb[:], reads=[XK], slot=("dbgx1", t % NXE))

    def e3(t):
        xb, hb2 = xe[t % NXE], h2b[t % NHE]
        XK = ("xe", t % NXE)
        jt, jkey = jk()
        act(jt, xb[:], AF.Square, [XK], [("st", "f_ss", t), jkey], accum=sc(ST2, 3, t))
        act(sc(ST2, 4, t), sc(ST2, 3, t), AF.Sqrt, [("st", "f_ss", t)], [("st", "f_sd", t)], scale=1.0 / D, bias=EPS)
        recip(sc(ST2, 5, t), sc(ST2, 4, t), [("st", "f_sd", t)], [("st", "f_rs", t)])
        stt(hb2[:], xb[:], sc(ST2, 5, t), g2[:, 0, :], ALU.mult, ALU.mult,
            [XK, ("st", "f_rs", t), ("g2", 0)], [("h2b", t % NHE)])

    def e4(t):
        hb2 = h2b[t % NHE]
        bank = 6 + t % 2
        pv = psb(bank).bitcast(BF16)
        for c in range(8):
            tr(pv[:, c * 128:(c + 1) * 128], hb2[:, c * 128:(c + 1) * 128], [("h2b", t % NHE), "ident"], psk(bank))
        src = pv.rearrange("p (c n) -> p c n", c=8)
        if t % 2:
            act(h2T[:, :, t * 128:(t + 1) * 128], src, AF.Copy, psk(bank), [("h2T", t)])
        else:
            cp("dve", h2T[:, :, t * 128:(t + 1) * 128], src, psk(bank), [("h2T", t)])

    dbgx1 = []
    drain(run_skewed([e1, e2, e3, e4], NT))
    P.dma("sp", g2[:, 1, :], gbc_d[:, 3, :], writes=[("g2", 1)], slot=("g2", 1))
    dump("h2T", h2T[:], [("h2T", t) for t in range(NT)])
    pop(R, N_E)
    pop(R, 5)

    uT = push(L, "uT", [128, NCP, 1024], BF16, ["uT"])
    tg = [push(L, "tg%d" % i, [128, 1024], F32, ["tgv"]) for i in range(2)]
    tv = [push(L, "tv%d" % i, [128, 1024], F32, ["tgv"]) for i in range(2)]
    gg = [push(L, "gg%d" % i, [128, 1024], F32, ["gg"]) for i in range(2)]
    xf = [push(R, "xf%d" % i, [128, D], F32, ["xf"]) for i in range(3)]
    yf = [push(R, "yf%d" % i, [128, D], F32, ["yf"]) for i in range(2)]

    def fw(ch, k):
        return pp[:, PP_FW + ch * 3 + k:PP_FW + ch * 3 + k + 1]

    def fb(ch):
        return pp[:, PP_FB + ch:PP_FB + ch + 1]

    step = 0

    for hf in range(2):
        for cpi in range(NCP):
            w = wu[step % 3]
            if step + 2 < 2 * NCP:
                wu_load(step + 2)
            base = 4 * (step % 2)
            for gv in range(2):
                for tbl in range(2):
                    b = base + 2 * gv + tbl
                    tb = 2 * hf + tbl
                    for c in range(8):
                        mm(psb(b), w[:, c, gv * 128:(gv + 1) * 128], h2T[:, c, tb * 512:(tb + 1) * 512],
                           c == 0, c == 7, [("h2T", 4 * tb + j) for j in range(4)] + [("wu", step % 3)], psk(b))
            bufs = (tg[step % 2], tv[step % 2])
            for gv in range(2):
                ch = cpi + NCP * gv
                G = psb(base + 2 * gv, 2)
                T = bufs[gv]
                kp = psk(base + 2 * gv, 2)
                tk = ("tgv", gv, step % 2)
                act(T[:], G, AF.Identity, kp + ["pp"], [tk], scale=fw(ch, 2), bias=fb(ch))
                if hf == 0:
                    act(fprev[:, ch, :], G[:, 1022:1024], AF.Copy, kp, [("fprev", ch)])
                stt(T[:, 1:1024], G[:, 0:1023], fw(ch, 1), T[:, 1:1024], ALU.mult, ALU.add, kp + [tk, "pp"], [tk])
                stt(T[:, 2:1024], G[:, 0:1022], fw(ch, 0), T[:, 2:1024], ALU.mult, ALU.add, kp + [tk, "pp"], [tk])
                if hf == 1:
                    stt(T[:, 0:2], fprev[:, ch, 0:2], fw(ch, 0), T[:, 0:2], ALU.mult, ALU.add,
                        [("fprev", ch), tk, "pp"], [tk])
                    stt(T[:, 0:1], fprev[:, ch, 1:2], fw(ch, 1), T[:, 0:1], ALU.mult, ALU.add,
                        [("fprev", ch), tk, "pp"], [tk])
            g_ = gg[step % 2]
            act(g_[:], bufs[0][:], AF.Gelu_apprx_tanh, [("tgv", 0, step % 2)], [("gg", step % 2)])
            tt("pool", uT[:, cpi, :], g_[:], bufs[1][:], ALU.mult, [("gg", step % 2), ("tgv", 1, step % 2)],
               [("uT", cpi)])
            step += 1
        for tl in range(8):
            t = hf * 8 + tl
            tsl = slice(t * 128, (t + 1) * 128)
            yb = 2 * (t % 4)
            xb, yy = xf[t % 3], yf[t % 2]
            XK = ("xf", t % 3)
            P.dma("sp", xb[:], out_d[tsl, :], reads=[("out", t)], writes=[XK], slot=("xf", t % 3))
            for dh in range(2):
                o = psb(yb + dh)
                for k in range(NCP):
                    mm(o, uT[:, k, tl * 128:(tl + 1) * 128], wdn[:, k, dh * 512:(dh + 1) * 512],
                       k == 0, k == NCP - 1, [("uT", k)] + WDN, psk(yb + dh))
            Y = psb(yb, 2)
            jt, jkey = jk()
            act(jt, Y, AF.Square, psk(yb, 2), [("st", "g_ss", t), jkey], accum=sc(ST3, 0, t))
            act(sc(ST3, 1, t), sc(ST3, 0, t), AF.Sqrt, [("st", "g_ss", t)], [("st", "g_sd", t)], scale=1.0 / D, bias=EPS)
            recip(sc(ST3, 2, t), sc(ST3, 1, t), [("st", "g_sd", t)], [("st", "g_rs", t)])
            stt(yy[:], Y, sc(ST3, 2, t), g2[:, 1, :], ALU.mult, ALU.mult,
                psk(yb, 2) + [("st", "g_rs", t), ("g2", 1)], [("yf", t % 2)])
            tt("dve", xb[:], xb[:], yy[:], ALU.add, [XK, ("yf", t % 2)], [XK])
            P.dma("sp", out_d[tsl, :], xb[:], reads=[XK], writes=[("out", t)], slot=("xoo", t % 3))

    dump("uT", uT[:], [("uT", k) for k in range(NCP)])
    P.emit()
    psc.__exit__(None, None, None)
    return nc


def _host_layout(inp):
    f = np.float32
    w_in = np.asarray(inp["w_in"][0], f)
    cols = []
    for fc in range(4):
        cols.append(np.arange(fc * 128, (fc + 1) * 128))
    for h in range(2):
        r = 512 + h * 64 + np.arange(64)
        cols.append(np.concatenate([r, r]))
    cols.append(640 + np.arange(128))
    cols.append(640 + np.arange(128))
    for c in range(4):
        cols.append(768 + c * 128 + np.arange(128))
    for c in range(4):
        cols.append(1280 + c * 128 + np.arange(128))
    w_in_r = np.stack([w_in[:, cc].reshape(8, 128, 128).transpose(1, 0, 2) for cc in cols]).astype(f)
    w_out_r = np.ascontiguousarray(np.asarray(inp["w_out"][0], f).reshape(8, 128, 1024).transpose(1, 0, 2))
    w_up = np.asarray(inp["w_up"][0], f)
    w_up_r = np.empty((NCP, 128, 8, 256), f)
    for cpi in range(NCP):
        g = w_up[:, cpi * 128:(cpi + 1) * 128].reshape(8, 128, 128).transpose(1, 0, 2)
        v = w_up[:, DFF + cpi * 128:DFF + (cpi + 1) * 128].reshape(8, 128, 128).transpose(1, 0, 2)
        w_up_r[cpi, :, :, 0:128] = g
        w_up_r[cpi, :, :, 128:256] = v
    w_down_r = np.ascontiguousarray(np.asarray(inp["w_down"][0], f).reshape(NCP, 128, 1024).transpose(1, 0, 2))

    def pc(v, n):
        return np.asarray(v, f).reshape(n, 128).T

    pp = np.zeros((128, NPP), f)
    cwt = np.asarray(inp["lru_conv_w"][0], f)
    for c in range(4):
        for k in range(4):
            pp[:, c * 4 + k] = cwt[k, c * 128:(c + 1) * 128]
    pp[:, 16:20] = pc(inp["lru_conv_b"][0], 4)
    pp[:, 20:24] = pc(inp["lru_ba"][0], 4)
    pp[:, 24:28] = pc(inp["lru_bx"][0], 4)
    pp[:, 28:32] = pc(inp["lru_lambda"][0], 4)
    fwt = np.asarray(inp["ffn_conv_w"][0], f)
    for ch in range(44):
        for k in range(3):
            pp[:, 32 + ch * 3 + k] = fwt[k, ch * 128:(ch + 1) * 128]
    pp[:, 164:208] = pc(inp["ffn_conv_b"][0], 44)
    pp[:, 208:212] = pc(inp["norm_attn_out"][0], 4)
    pp[:, 212:216] = pc(inp["norm_lru_out"][0], 4)

    gbc = np.empty((128, 4, 1024), f)
    for i, k in enumerate(["norm_mix_pre", "norm_mix_post", "norm_ffn_pre", "norm_ffn_post"]):
        gbc[:, i, :] = np.asarray(inp[k][0], f)[None, :]

    btab = np.empty((128, 2, 2, 4, 128), f)
    kk = np.arange(128)[:, None]
    qq = np.arange(128)[None, :]
    for h in range(2):
        for rt in range(2):
            for slot in range(4):
                j, prev = slot % 2, slot // 2
                hq = 4 * h + 2 * j + rt
                slope = 2.0 ** (-(hq + 1))
                dist = (qq - kk) + (128 if prev else 0)
                valid = (dist >= 0) & (dist < 128)
                btab[:, h, rt, slot, :] = np.where(valid, -slope * dist, -30000.0)
    sinks = np.asarray(inp["sinks"][0], f)
    sinkbc = np.ascontiguousarray(np.broadcast_to(sinks[None, :], (128, 8))).astype(f)

    def bd(w):
        w = np.asarray(w[0], f)
        o = np.zeros((128, 4, 128), f)
        for c in range(4):
            for hh in range(2):
                o[hh * 64:(hh + 1) * 64, c, hh * 64:(hh + 1) * 64] = w[2 * c + hh]
        return o

    shared = {
        "w_in_r": w_in_r, "w_out_r": w_out_r, "w_up_r": w_up_r, "w_down_r": w_down_r,
        "pp": pp, "gbc": gbc, "btab": btab.reshape(128, 2048), "sinkbc": sinkbc,
        "wab": bd(inp["lru_wa"]), "wxb": bd(inp["lru_wx"]), "ident": np.eye(128, dtype=f),
    }
    return shared


_NC_CACHE = {}


def kernel(**inputs):
    x = np.asarray(inputs["x"], np.float32)
    shared = _host_layout(inputs)
    if "nc" not in _NC_CACHE:
        _NC_CACHE["nc"] = build_nc()
    nc = _NC_CACHE["nc"]
    in_maps = []
    for b in range(8):
        m = dict(shared)
        m["x"] = np.ascontiguousarray(x[b])
        in_maps.append(m)
    res = run_bass_kernel_spmd(nc, in_maps, core_ids=list(range(8)))
    return np.stack([np.asarray(r["out"], np.float32) for r in res.results], axis=0)
```

```python
import bisect
import contextlib
import numpy as np
import concourse.bass as bass
import concourse.mybir as mybir
from concourse.bass_utils import run_bass_kernel_spmd

F32 = mybir.dt.float32
BF16 = mybir.dt.bfloat16
AF = mybir.ActivationFunctionType
ALU = mybir.AluOpType

S = 2048
D = 1024
NT = 16
NB = 4
DFF = 2816
NCP = 22
EPS = 1e-6
NPP = 216

ENGS = ("pe", "act", "dve", "pool", "sp")


class Op:
    __slots__ = ("eng", "fn", "reads", "writes", "deps", "idx", "is_dma", "slot", "sig", "val")

    def __init__(self, eng, fn, reads, writes, is_dma, slot):
        self.eng, self.fn, self.reads, self.writes = eng, fn, reads, writes
        self.is_dma, self.slot = is_dma, slot
        self.deps = set()
        self.sig = False
        self.val = None


class Prog:
    def __init__(self, nc):
        self.nc = nc
        self.ops = []
        self.last_writer = {}
        self.readers = {}
        self.pending = {}

    @staticmethod
    def fam(k):
        return k if isinstance(k, str) else k[0]

    def frontier(self, fams):
        fams = set(fams)
        out = set()
        for k, j in self.last_writer.items():
            if self.fam(k) in fams:
                out.add(j)
        for k, js in self.readers.items():
            if self.fam(k) in fams:
                out.update(js)
        return out

    def op(self, eng, fn, reads=(), writes=(), dma=False, slot=None):
        o = Op(eng, fn, tuple(reads), tuple(writes), dma, slot)
        o.idx = len(self.ops)
        lw, rd = self.last_writer, self.readers
        if self.pending:
            for k in o.reads + o.writes:
                pd = self.pending.get(self.fam(k))
                if pd:
                    o.deps.update(pd)
        for k in o.reads:
            j = lw.get(k)
            if j is not None:
                o.deps.add(j)
            if self.fam(k) == "ps":
                for r in rd.get(k, ()):
                    if self.ops[r].eng != eng:
                        o.deps.add(r)
        for k in o.writes:
            j = lw.get(k)
            if j is not None:
                o.deps.add(j)
            for r in rd.get(k, ()):
                o.deps.add(r)
        for k in o.reads:
            rd.setdefault(k, []).append(o.idx)
        for k in o.writes:
            lw[k] = o.idx
            rd[k] = []
        o.deps.discard(o.idx)
        self.ops.append(o)
        return o

    def dma(self, eng, out, in_, reads=(), writes=(), slot=None):
        return self.op(eng, lambda e: e.dma_start(out=out, in_=in_), reads, writes,
                       dma=True, slot=slot)

    def emit(self):
        nc = self.nc
        ops = self.ops
        for o in ops:
            for j in o.deps:
                p = ops[j]
                if p.eng == "pe" and o.eng == "pe" and not p.is_dma:
                    continue
                p.sig = True
        slots = sorted({o.slot for o in ops if o.is_dma}, key=str)
        st = contextlib.ExitStack()
        with st:
            esem = {e: st.enter_context(nc.semaphore("s_" + e)) for e in ENGS}
            ssem = {s: st.enter_context(nc.semaphore("d_%s" % (str(s),))) for s in slots}
            ecnt = {e: 0 for e in ENGS}
            scnt = {s: 0 for s in slots}
            slot_hist = {s: [] for s in slots}
            for o in ops:
                if o.is_dma:
                    scnt[o.slot] += 16
                    o.val = scnt[o.slot]
                    slot_hist[o.slot].append((o.idx, o.val))
                elif o.sig:
                    ecnt[o.eng] += 1
                    o.val = ecnt[o.eng]
            per_eng = {e: [o for o in ops if o.eng == e] for e in ENGS}

            def slot_val_before(s, idx):
                h = slot_hist[s]
                k = bisect.bisect_left(h, (idx, -1))
                return h[k - 1][1] if k > 0 else 0

            block = st.enter_context(nc.Block())

            def make(e):
                def body(eng):
                    waited = {}
                    for o in per_eng[e]:
                        need = {}
                        for j in o.deps:
                            p = ops[j]
                            if p.is_dma:
                                key = ("s", p.slot)
                                v = slot_val_before(p.slot, o.idx)
                            else:
                                if p.eng == "pe" and o.eng == "pe":
                                    continue
                                key = ("e", p.eng)
                                v = p.val
                            if v > need.get(key, 0):
                                need[key] = v
                        for key, v in need.items():
                            if waited.get(key, 0) >= v:
                                continue
                            waited[key] = v
                            sem = ssem[key[1]] if key[0] == "s" else esem[key[1]]
                            eng.wait_ge(sem, v)
                        ins = o.fn(eng)
                        if o.is_dma:
                            ins.then_inc(ssem[o.slot], 16)
                        elif o.sig:
                            ins.then_inc(esem[o.eng], 1)
                    if e == "sp":
                        for s in slots:
                            if waited.get(("s", s), 0) < scnt[s]:
                                eng.wait_ge(ssem[s], scnt[s])
                return body

            block.tensor(make("pe"))
            block.scalar(make("act"))
            block.vector(make("dve"))
            block.gpsimd(make("pool"))
            block.sync(make("sp"))


def build_nc(debug=False):
    nc = bass.Bass("TRN2", target_bir_lowering=False)
    dbg_n = [0]

    def din(name, shape):
        return nc.dram_tensor(name, list(shape), F32, kind="ExternalInput").ap()

    x_d = din("x", [S, D])
    win_d = din("w_in_r", [16, 128, 8, 128])
    wout_d = din("w_out_r", [128, 8, 1024])
    wup_d = din("w_up_r", [NCP, 128, 8, 256])
    wdn_d = din("w_down_r", [128, NCP, 1024])
    pp_d = din("pp", [128, NPP])
    gbc_d = din("gbc", [128, 4, 1024])
    btab_d = din("btab", [128, 2048])
    sink_d = din("sinkbc", [128, 8])
    wa_d = din("wab", [128, 4, 128])
    wx_d = din("wxb", [128, 4, 128])
    id_d = din("ident", [128, 128])
    out_d = nc.dram_tensor("out", [S, D], F32, kind="ExternalOutput").ap()

    P = Prog(nc)

    stacks = {"left": [], "right": []}
    retired = {"left": set(), "right": set()}
    ptr = {"left": 16512, "right": 229376 - 64}

    def push(side, name, shape, dt, fams):
        nbytes = int(np.prod(shape[1:])) * (4 if dt == F32 else 2)
        nbytes = (nbytes + 63) // 64 * 64
        if side == "left":
            off = ptr["left"]
            ptr["left"] += nbytes
        else:
            ptr["right"] -= nbytes
            off = ptr["right"]
        assert ptr["left"] <= ptr["right"], ("SBUF overflow", name, ptr)
        t = nc.alloc_sbuf_tensor_at(name, list(shape), dt, offset=off)
        stacks[side].append((nbytes, tuple(fams)))
        allret = retired["left"] | retired["right"]
        if allret:
            fs = frozenset(allret)
            for f in fams:
                P.pending[f] = fs
        return t

    def pop(side, n):
        for _ in range(n):
            nbytes, fams = stacks[side].pop()
            retired[side] |= P.frontier(fams)
            if side == "left":
                ptr["left"] -= nbytes
            else:
                ptr["right"] += nbytes

    L, R = "left", "right"

    pp = push(L, "pp", [128, NPP], F32, ["pp"])
    g2 = push(L, "g2", [128, 2, 1024], F32, ["g2"])
    ident = push(L, "ident", [128, 128], BF16, ["ident"])
    ones = push(L, "ones", [128, 128], BF16, ["ones"])
    wab = push(L, "wab", [128, 4, 128], BF16, ["wab"])
    wxb = push(L, "wxb", [128, 4, 128], BF16, ["wxb"])
    dp = push(L, "dp", [128, 64], F32, ["dp"])
    junk2 = push(L, "junk", [128, 3, 1024], BF16, ["junk"])
    jcnt = [0]

    def jk(n_=1024):
        i = jcnt[0] % 3
        jcnt[0] += 1
        return junk2[:, i, 0:n_], ("junk", i)
    stats = push(L, "stats", [128, 256], F32, ["st"])
    fprev = push(L, "fprev", [128, 2 * NCP, 2], F32, ["fprev"])
    esink = push(L, "esink", [128, 8], F32, ["esink"])
    psc = nc.psum_tensor("ps", [128, 8, 512], F32)
    ps = psc.__enter__()

    def psb(b, n=1):
        return ps[:, b:b + n, :].rearrange("p b n -> p (b n)")

    def psk(b, n=1):
        return [("ps", b + i) for i in range(n)]

    def act(out, in_, func, reads, writes, scale=1.0, bias=0.0, accum=None):
        kw = dict(out=out, in_=in_, func=func, scale=scale, bias=bias)
        if accum is not None:
            kw["accum_out"] = accum
        return P.op("act", lambda e: e.activation(**kw), reads, writes)

    def tt(eng, out, in0, in1, op, reads, writes):
        return P.op(eng, lambda e: e.tensor_tensor(out=out, in0=in0, in1=in1, op=op), reads, writes)

    def stt(out, in0, scalar, in1, op0, op1, reads, writes):
        return P.op("dve", lambda e: e.scalar_tensor_tensor(out=out, in0=in0, scalar=scalar, in1=in1,
                                                            op0=op0, op1=op1), reads, writes)

    def ts(eng, out, in0, s1, s2, op0, op1, reads, writes):
        return P.op(eng, lambda e: e.tensor_scalar(out=out, in0=in0, scalar1=s1, scalar2=s2,
                                                   op0=op0, op1=op1), reads, writes)

    def cp(eng, out, in_, reads, writes):
        return P.op(eng, lambda e: e.tensor_copy(out=out, in_=in_), reads, writes)

    def mm(out, lhsT, rhs, start, stop, reads, writes):
        return P.op("pe", lambda e: e.matmul(out, lhsT=lhsT, rhs=rhs, start=start, stop=stop),
                    reads, writes)

    def tr(out, in_, reads, writes):
        return P.op("pe", lambda e: e.transpose(out, in_, ident[:]), reads, writes)

    def recip(out, in_, reads, writes):
        return P.op("dve", lambda e: e.reciprocal(out=out, in_=in_), reads, writes)

    def dump(name, t, keys):
        if not debug:
            return
        d = nc.dram_tensor("dbg_" + name, list(t.shape), t.dtype, kind="ExternalOutput").ap()
        dbg_n[0] += 1
        P.dma("sp", d, t, reads=keys, slot=("dbg", dbg_n[0]))

    P.dma("sp", pp[:], pp_d[:, :], writes=["pp"], slot="c0")
    P.dma("sp", g2[:, 0, :], gbc_d[:, 0, :], writes=[("g2", 0)], slot=("g2", 0))
    P.dma("sp", g2[:, 1, :], gbc_d[:, 1, :], writes=[("g2", 1)], slot=("g2", 1))
    P.dma("pool", ident[:], id_d[:, :], writes=["ident"], slot="c2")
    P.dma("pool", wab[:], wa_d[:, :, :], writes=["wab"], slot="c3")
    P.dma("pool", wxb[:], wx_d[:, :, :], writes=["wxb"], slot="c4")
    P.dma("sp", esink[:], sink_d[:, :], writes=["esink"], slot="c6")
    P.op("dve", lambda e: e.memset(ones[:], 1.0), writes=["ones"])

    def cw(c, k):
        return pp[:, c * 4 + k:c * 4 + k + 1]
    PP_CB, PP_BA, PP_BX, PP_LAM, PP_FW, PP_FB, PP_GM = 16, 20, 24, 28, 32, 164, 208
    DP_CV, DP_CH, DP_HBA, DP_HBX = 0, 4, 8, 12

    lam = pp[:, PP_LAM:PP_LAM + 4]
    t_ax, t_e, t_l, t_s, t_m, t_mx, t_nl, t_c = (dp[:, 16 + 4 * i:20 + 4 * i] for i in range(8))
    P.op("dve", lambda e: e.memset(t_c, 0.1), writes=["d_c"])
    ts("dve", t_nl, lam, -1.0, None, ALU.mult, ALU.bypass, ["pp"], ["d_nl"])
    tt("dve", t_ax, lam, t_nl, ALU.max, ["pp", "d_nl"], ["d_ax"])
    act(t_e, t_ax, AF.Exp, ["d_ax"], ["d_e"], scale=-1.0)
    act(t_l, t_e, AF.Ln, ["d_e"], ["d_l"], bias=1.0)
    act(esink[:], esink[:], AF.Exp, ["esink"], ["esink"])
    ts("dve", t_s, t_e, -1.0 / 3.0, 0.5, ALU.mult, ALU.add, ["d_e"], ["d_s"])
    tt("dve", t_s, t_s, t_e, ALU.mult, ["d_s", "d_e"], ["d_s"])
    ts("dve", t_s, t_s, -1.0, 1.0, ALU.mult, ALU.add, ["d_s"], ["d_s"])
    tt("dve", t_s, t_s, t_e, ALU.mult, ["d_s", "d_e"], ["d_s"])
    tt("dve", t_m, t_e, t_c, ALU.is_lt, ["d_e", "d_c"], ["d_m"])
    tt("dve", t_s, t_s, t_l, ALU.subtract, ["d_s", "d_l"], ["d_s"])
    tt("dve", t_s, t_s, t_m, ALU.mult, ["d_s", "d_m"], ["d_s"])
    tt("dve", t_l, t_l, t_s, ALU.add, ["d_s", "d_l"], ["d_l"])
    ts("dve", t_mx, t_nl, 0.0, None, ALU.max, ALU.bypass, ["d_nl"], ["d_mx"])
    tt("dve", t_l, t_l, t_mx, ALU.add, ["d_l", "d_mx"], ["d_l"])
    ts("dve", dp[:, DP_CV:DP_CV + 4], t_l, -8.0, None, ALU.mult, ALU.bypass, ["d_l"], ["d_cv"])
    ts("dve", dp[:, DP_CH:DP_CH + 4], t_l, -4.0, None, ALU.mult, ALU.bypass, ["d_l"], ["d_ch"])
    ts("dve", dp[:, DP_HBA:DP_HBA + 4], pp[:, PP_BA:PP_BA + 4], 0.5, None, ALU.mult, ALU.bypass, ["pp"], ["d_hba"])
    ts("dve", dp[:, DP_HBX:DP_HBX + 4], pp[:, PP_BX:PP_BX + 4], 0.5, None, ALU.mult, ALU.bypass, ["pp"], ["d_hbx"])
    DPK = ["d_cv", "d_ch", "d_hba", "d_hbx"]

    ST_A = 0
    ST2 = 48
    ST3 = 144
    ST_N = 192

    def sc(base, k, t):
        return stats[:, base + 16 * k + t:base + 16 * k + t + 1]

    def run_skewed(stages, n):
        ns = len(stages)
        for step in range(n + ns - 1):
            for k, stg in enumerate(stages):
                t = step - k
                if 0 <= t < n:
                    stg(t)
                    yield

    def drain(gen):
        for _ in gen:
            pass

    lruT = push(R, "lruT", [128, 4, S], BF16, ["lruT", "lruT0"])
    lxp = push(L, "lxp", [128, 4, S + 4], BF16, ["lx", "lxpad"])
    glT = push(L, "glT", [128, 4, S], BF16, ["gl"])
    qT = push(L, "qT", [128, 4, S], BF16, ["qT"])
    kT = push(L, "kT", [128, 2, S], BF16, ["kT"])
    vaug = push(L, "vaug", [128, NT, 2, 66], BF16, ["v", "vone"])
    dset = []
    sqb = None

    def D_ALLOC():
        nonlocal sqb
        for i in range(2):
            dset.append(dict(
                xc=push(R, "xc%d" % i, [128, S], F32, ["xc"]),
                xcb=push(R, "xcb%d" % i, [128, S // 2], BF16, ["xcb"]),
                tha=push(R, "tha%d" % i, [128, S], F32, ["tha"]),
                thx=push(R, "thx%d" % i, [128, S], F32, ["thx"]),
                a2=push(R, "a2%d" % i, [128, S], F32, ["a2"])))
        sqb = [push(R, "sqb%d" % i, [128, 512], BF16, ["sqb"]) for i in range(2)]
    N_D = 10 + 2
    hT = push(L, "hT", [128, 8, S], BF16, ["hT"])
    wc = [push(L, "wc%d" % i, [128, 8, 128], BF16, ["wc"]) for i in range(3)]
    wv = push(L, "wv", [128, 8, 128], BF16, ["wv"])
    N_B = 1 + 3 + 1
    NXT, NHB = 8, 4
    xt = [push(L, "xt%d" % i, [128, D], F32, ["xt"]) for i in range(NXT)]
    hb = [push(L, "hb%d" % i, [128, D], BF16, ["hb"]) for i in range(NHB)]
    N_A = NXT + NHB

    P.op("pool", lambda e: e.memset(lxp[:, :, 0:4], 0.0), writes=[("lxpad",)])
    P.op("pool", lambda e: e.memset(vaug[:, :, :, 64:66], 1.0), writes=[("vone",)])

    def a1(t):
        xb = xt[t % NXT]
        P.dma("sp", xb[:], x_d[t * 128:(t + 1) * 128, :], writes=[("xt", t % NXT)], slot=("xt", t % NXT))
        jt, jkey = jk()
        act(jt, xb[:], AF.Square, [("xt", t % NXT)], [("st", "ssA", t), jkey], accum=sc(ST_A, 0, t))

    def a2_(t):
        xb, hbb = xt[t % NXT], hb[t % NHB]
        act(sc(ST_A, 1, t), sc(ST_A, 0, t), AF.Sqrt, [("st", "ssA", t)], [("st", "sdA", t)], scale=1.0 / D, bias=EPS)
        recip(sc(ST_A, 2, t), sc(ST_A, 1, t), [("st", "sdA", t)], [("st", "rsA", t)])
        stt(hbb[:], xb[:], sc(ST_A, 2, t), g2[:, 0, :], ALU.mult, ALU.mult,
            [("xt", t % NXT), ("st", "rsA", t), ("g2", 0)], [("hb", t % NHB)])

    def a3(t):
        hbb = hb[t % NHB]
        bank = t % 2
        pv = psb(bank).bitcast(BF16)
        for c in range(8):
            tr(pv[:, c * 128:(c + 1) * 128], hbb[:, c * 128:(c + 1) * 128],
               [("hb", t % NHB), "ident"], psk(bank))
        cp("dve", hT[:, :, t * 128:(t + 1) * 128], pv.rearrange("p (c n) -> p c n", c=8), psk(bank), [("hT", t)])

    def hTk(tb):
        return [("hT", 4 * tb + j) for j in range(4)]

    bank_rr = [2, 3, 4, 5]
    bank_rr6 = [2, 3, 4, 5, 0, 1]
    use6 = [False]
    nbc = [0]
    evc = [0]

    def b_unit(w, wkey, fc, tb):
        b = (bank_rr6[nbc[0] % 6] if use6[0] else bank_rr[nbc[0] % 4]); nbc[0] += 1
        o = psb(b)
        sl = slice(tb * 512, (tb + 1) * 512)
        for c in range(8):
            mm(o, w[:, c, :], hT[:, c, sl], c == 0, c == 7, hTk(tb) + [wkey], psk(b))
        if fc < 4:
            act(qT[:, fc, sl], o, AF.Copy, psk(b), [("qT", fc, tb)], scale=0.125)
        elif fc < 6:
            cp("dve", kT[:, fc - 4, sl], o, psk(b), [("kT", fc - 4, tb)])
        elif fc < 12:
            c4 = fc - 8
            cp("dve", lxp[:, c4, 4 + tb * 512:4 + (tb + 1) * 512], o, psk(b), [("lx", c4, tb)])
        else:
            c4 = fc - 12
            act(glT[:, c4, sl], o, AF.Gelu_apprx_tanh, psk(b), [("gl", c4, tb)])

    def v_unit(t0):
        b = (bank_rr6[nbc[0] % 6] if use6[0] else bank_rr[nbc[0] % 4]); nbc[0] += 1
        for j in range(4):
            t = t0 + j
            o = psb(b)[:, j * 128:(j + 1) * 128]
            for c in range(8):
                mm(o, hT[:, c, t * 128:(t + 1) * 128], wv[:, c, :], c == 0, c == 7, [("hT", t), "wv"], psk(b))
        dst = vaug[:, t0:t0 + 4, :, 0:64]
        src_ = psb(b).rearrange("p (t h m) -> p t h m", t=4, h=2)
        keys = [("v", t0 + j) for j in range(4)]
        if (t0 // 4) % 2:
            cp("dve", dst, src_, psk(b), keys)
        else:
            act(dst, src_, AF.Copy, psk(b), keys)

    order = [8, 9, 10, 11, 12, 13, 14, 15, 4, 5, 0, 1, 2, 3]
    P.dma("pool", wv[:], win_d[6], writes=["wv"], slot="wv")
    P.dma("pool", wc[0][:], win_d[order[0]], writes=[("wc", 0)], slot=("wc", 0))

    def b_gen():
        for n, fc in enumerate(order):
            w = wc[n % 3]
            if n > 0:
                P.dma("pool", w[:], win_d[fc], writes=[("wc", n % 3)], slot=("wc", n % 3))
            for tb in range(NB):
                b_unit(w, ("wc", n % 3), fc, tb)
                yield (n, tb)
            if n == 9:
                for t in range(0, NT, 4):
                    v_unit(t)
                    yield (n, 0)

    etab = ex = PT = dsm = atok = None

    def C_ALLOC():
        nonlocal etab, ex, PT, dsm, atok
        etab = push(L, "etab", [128, 2, 1024], BF16, ["etab"])
        ex = []
        PT = [push(L, "PT%d" % i, [128, 1024], BF16, ["PT"]) for i in range(2)]
        dsm = push(L, "dsm", [128, 2, 4], F32, ["dsm"])
        atok = push(L, "atok", [128, NT, 512], BF16, ["atok"])
        P.dma("pool", etab[:].rearrange("p h n -> p (h n)"), btab_d[:, :], writes=["etab"], slot="c5")
    N_C = 1 + 2 + 1 + 1

    def S_step(s):
        i, h = s // 2, s % 2
        base = 2 * (s % 2)
        tb = i // 4
        etv = etab[:, h, :].rearrange("p (r n) -> p r n", r=2)
        for rt in range(2):
            b = base + rt
            o = psb(b).rearrange("p (s q) -> p s q", s=4)
            rows = slice(rt * 64, (rt + 1) * 64)
            mm(psb(b), ident[:], etv[:, rt, :], True, False, ["ident", "etab"], psk(b))
            for j in range(2):
                qr = qT[rows, 2 * h + j, i * 128:(i + 1) * 128]
                last = (j == 1)
                mm(o[:, j, :], kT[rows, h, i * 128:(i + 1) * 128], qr, False, last and i == 0,
                   [("kT", h, tb), ("qT", 2 * h + j, tb)], psk(b))
                if i > 0:
                    mm(o[:, 2 + j, :], kT[rows, h, (i - 1) * 128:i * 128], qr, False, last,
                       [("kT", h, (i - 1) // 4), ("qT", 2 * h + j, tb)], psk(b))

    def view4(a):
        return a.rearrange("p (r s q) -> p r s q", r=2, s=4)

    def E_step(s):
        base = 2 * (s % 2)
        act(PT[s % 2][:], psb(base, 2), AF.Exp, psk(base, 2), [("PT", s % 2)])

    def PV_step(s):
        i, h = s // 2, s % 2
        ptv = view4(PT[s % 2][:])
        ob = 4 + s % 2
        o4 = psb(ob).rearrange("p (g n) -> p g n", g=4)
        for g in range(4):
            rt, j = g % 2, g // 2
            mm(o4[:, g, 0:65], ptv[:, rt, j, :], vaug[:, i, h, 0:65], True, i == 0,
               [("PT", s % 2), ("v", i), ("vone",)], psk(ob))
            if i > 0:
                mm(o4[:, g, 0:65], ptv[:, rt, 2 + j, :], vaug[:, i - 1, h, 0:65], False, True,
                   [("PT", s % 2), ("v", i - 1), ("vone",)], psk(ob))
        d = dsm[:, s % 2, :]
        tt("dve", d, o4[:, :, 64], esink[:, 4 * h:4 * h + 4], ALU.add, psk(ob) + ["esink"], [("dsm", s % 2)])
        recip(d, d, [("dsm", s % 2)], [("dsm", s % 2)])
        for g in range(4):
            hq = 4 * h + g
            ts("dve", atok[:, i, hq * 64:(hq + 1) * 64], o4[:, g, 0:64], dsm[:, s % 2, g:g + 1], None,
               ALU.mult, ALU.bypass, psk(ob) + [("dsm", s % 2)], [("atok", i, hq)])

    def c_gen():
        NS = 2 * NT
        S_step(0)
        for s in range(NS):
            if s + 1 < NS:
                S_step(s + 1)
            E_step(s)
            yield
            PV_step(s)
            if s % 2 == 1:
                t_ = s // 2
                jt, jkey = jk(512)
                act(jt, atok[:, t_, :], AF.Square, [("atok", t_, hq) for hq in range(8)],
                    [("st", "ssN", t_), jkey], accum=sc(ST_N, 0, t_))
            yield

    def chunk_gen(c, si):
        H2 = S // 2
        B_ = dset[si]
        xc, xcb, tha, thx, a2 = B_["xc"], B_["xcb"], B_["tha"], B_["thx"], B_["a2"]
        hs = [slice(0, H2), slice(H2, S)]
        LX = [("lx", c, tb) for tb in range(NB)] + [("lxpad",)]
        for hf in range(2):
            ts("dve", xc[:, hs[hf]], lxp[:, c, 4 + hf * H2:4 + (hf + 1) * H2], cw(c, 3),
               pp[:, PP_CB + c:PP_CB + c + 1], ALU.mult, ALU.add, LX + ["pp"], [("xc", si, hf)])
            yield
        for k in (2, 1, 0):
            for hf in range(2):
                stt(xc[:, hs[hf]], lxp[:, c, 1 + k + hf * H2:1 + k + (hf + 1) * H2], cw(c, k), xc[:, hs[hf]],
                    ALU.mult, ALU.add, LX + ["pp", ("xc", si, hf)], [("xc", si, hf)])
                yield
        for tb in range(NB):
            sl = slice(tb * 512, (tb + 1) * 512)
            if tb % 2 == 0:
                cp("dve", xcb[:], xc[:, hs[tb // 2]], [("xc", si, tb // 2)], [("xcb", si)])
                yield
            sl2 = slice((tb % 2) * 512, (tb % 2 + 1) * 512)
            mm(psb(6), wab[:, c, :], xcb[:, sl2], True, True, [("xcb", si), "wab"], psk(6))
            mm(psb(7), wxb[:, c, :], xcb[:, sl2], True, True, [("xcb", si), "wxb"], psk(7))
            act(tha[:, sl], psb(6), AF.Tanh, psk(6) + DPK, [("tha", si, tb)], scale=0.5,
                bias=dp[:, DP_HBA + c:DP_HBA + c + 1])
            act(thx[:, sl], psb(7), AF.Tanh, psk(7) + DPK, [("thx", si, tb)], scale=0.5,
                bias=dp[:, DP_HBX + c:DP_HBX + c + 1])
            yield
        for hf in range(2):
            hk = [("tha", si, 2 * hf), ("tha", si, 2 * hf + 1)]
            act(a2[:, hs[hf]], tha[:, hs[hf]], AF.Exp, hk + DPK, [("a2", si, hf)], scale=dp[:, DP_CV + c:DP_CV + c + 1],
                bias=dp[:, DP_CV + c:DP_CV + c + 1])
            act(tha[:, hs[hf]], tha[:, hs[hf]], AF.Exp, hk + DPK, hk, scale=dp[:, DP_CH + c:DP_CH + c + 1],
                bias=dp[:, DP_CH + c:DP_CH + c + 1])
            yield
            xk = [("thx", si, 2 * hf), ("thx", si, 2 * hf + 1)]
            stt(thx[:, hs[hf]], thx[:, hs[hf]], 1.0, xc[:, hs[hf]], ALU.add, ALU.mult, xk + [("xc", si, hf)], xk)
            yield
        A2 = [("a2", si, 0), ("a2", si, 1)]
        act(a2[:], a2[:], AF.Sqrt, A2, A2, scale=-1.0, bias=1.0)
        yield
        for hf in range(2):
            xk = [("thx", si, 2 * hf), ("thx", si, 2 * hf + 1)]
            stt(thx[:, hs[hf]], thx[:, hs[hf]], 0.5, a2[:, hs[hf]], ALU.mult, ALU.mult, xk + [("a2", si, hf)], xk)
            yield
        for hf in range(2):
            hk = [("tha", si, 2 * hf), ("tha", si, 2 * hf + 1)]
            xk = [("thx", si, 2 * hf), ("thx", si, 2 * hf + 1)]
            init = 0.0 if hf == 0 else a2[:, H2 - 1:H2]
            P.op("dve", lambda e, hf=hf, init=init: e.tensor_tensor_scan(
                out=a2[:, hs[hf]], data0=tha[:, hs[hf]], data1=thx[:, hs[hf]], initial=init,
                op0=ALU.mult, op1=ALU.add), hk + xk + ([("a2", si, 0)] if hf else []), [("a2", si, hf)])
            yield
        for hf in range(2):
            tt("dve", lruT[:, c, hs[hf]], a2[:, hs[hf]], glT[:, c, hs[hf]], ALU.mult,
               [("a2", si, hf)] + [("gl", c, tb) for tb in range(NB)], [("lruT", c)] if hf else [("lruT0", c)])
            yield

    def d_gen():
        def chain2(c0, c1, si):
            for g_ in (chunk_gen(c0, si), chunk_gen(c1, si)):
                for _ in g_:
                    yield
        gens = [chain2(0, 2, 0), chain2(1, 3, 1)]
        alive = [True, True]
        for _ in range(12):
            next(gens[0])
            yield
        while alive[0] or alive[1]:
            for gi in range(2):
                if alive[gi]:
                    try:
                        next(gens[gi])
                        yield
                    except StopIteration:
                        alive[gi] = False
        xc = dset[0]["xc"]
        rbc = xc
        n = 0
        for tb in range(NB):
            sl = slice(tb * 512, (tb + 1) * 512)
            bank = 6 + tb % 2
            for c in range(4):
                sb_ = sqb[n % 2]
                tt("pool", sb_[:], lruT[:, c, sl], lruT[:, c, sl], ALU.mult, [("lruT", c), ("lruT0", c)], [("sqb", n % 2)])
                mm(psb(bank), ones[:], sb_[:], c == 0, c == 3, [("sqb", n % 2), "ones"], psk(bank))
                n += 1
                yield
            act(rbc[:, sl], psb(bank), AF.Ln, psk(bank), [("xc", 0, tb // 2)], scale=1.0 / 512, bias=EPS)
            yield
        act(rbc[:], rbc[:], AF.Exp, [("xc", 0, 0), ("xc", 0, 1)], [("xc", 0, 0), ("xc", 0, 1)], scale=-0.5)
        yield
        for c in range(4):
            tt("dve", lruT[:, c, :], lruT[:, c, :], rbc[:], ALU.mult,
               [("xc", 0, 0), ("xc", 0, 1), ("lruT", c), ("lruT0", c)], [("lruT", c), ("lruT0", c)])
            yield

    def step_gen(g, n=1):
        for _ in range(n):
            try:
                next(g)
            except StopIteration:
                return False
        return True

    gb = b_gen()
    gd = d_gen()
    ns = 3
    for stepi in range(NT + ns - 1):
        for k, stg in enumerate([a1, a2_, a3]):
            t = stepi - k
            if 0 <= t < NT:
                stg(t)
                if k == 2 and t % 4 == 3:
                    next(gb)
    pop(L, N_A)
    D_ALLOC()
    use6[0] = True
    d_alive = True
    vcnt = 0
    for (n, tb) in gb:
        if n >= 8 and d_alive:
            if tb == 4:
                vcnt += 1
                if vcnt % 2 == 0:
                    d_alive = step_gen(gd, 1)
            else:
                d_alive = step_gen(gd, 2)
    P.dma("sp", g2[:, 0, :], gbc_d[:, 2, :], writes=[("g2", 0)], slot=("g2", 0))
    dump("hT", hT[:], [("hT", t) for t in range(NT)])
    dump("qT", qT[:], [("qT", a, b) for a in range(4) for b in range(NB)])
    dump("kT", kT[:], [("kT", a, b) for a in range(2) for b in range(NB)])
    dump("lxp", lxp[:], [("lx", a, b) for a in range(4) for b in range(NB)] + [("lxpad",)])
    dump("glT", glT[:], [("gl", a, b) for a in range(4) for b in range(NB)])
    pop(L, N_B)
    C_ALLOC()
    gc = c_gen()
    c_alive = True
    while c_alive or d_alive:
        if c_alive:
            c_alive = step_gen(gc)
        if d_alive:
            d_alive = step_gen(gd, 2)
    ATOK = [("atok", i, hq) for i in range(NT) for hq in range(8)]
    dump("lruN", lruT[:], [("lruT", c) for c in range(4)])
    dump("xc3", dset[1]["xc"][:], [("xc", 1, 0), ("xc", 1, 1)])
    dump("tha3", dset[1]["tha"][:], [("tha", 1, tb) for tb in range(NB)])
    dump("u3", dset[1]["thx"][:], [("thx", 1, tb) for tb in range(NB)])
    dump("h3", dset[1]["a2"][:], [("a2", 1, 0), ("a2", 1, 1)])
    dump("atok", atok[:], ATOK)
    pop(R, N_D)

    attnT = push(R, "attnT", [128, 4, S], BF16, ["attnT"])
    wo32 = [push(R, "wo32_%d" % i, [128, 2, 1024], F32, ["wo32"]) for i in range(2)]
    wob = push(R, "wob", [128, 8, 1024], BF16, ["wob"])
    for i in range(4):
        P.dma("sp", wo32[i % 2][:], wout_d[:, 2 * i:2 * i + 2, :], writes=[("wo32", i % 2)], slot=("wo32", i % 2))
        for cc in range(2):
            c = 2 * i + cc
            ts("pool", wob[:, c, :], wo32[i % 2][:, cc, :], pp[:, PP_GM + c:PP_GM + c + 1], 1.0, ALU.mult, ALU.mult,
               [("wo32", i % 2), "pp"], [("wob", c)])
    anb = [push(L, "anb%d" % i, [128, 512], BF16, ["anb"]) for i in range(3)]
    SSN = [("st", "ssN", t) for t in range(NT)]
    act(stats[:, ST_N + 16:ST_N + 32], stats[:, ST_N:ST_N + 16], AF.Sqrt, SSN, [("st", "sdN")], scale=1.0 / 512, bias=EPS)
    recip(stats[:, ST_N + 32:ST_N + 48], stats[:, ST_N + 16:ST_N + 32], [("st", "sdN")], [("st", "rsN")])

    def n1(t):
        ts("dve", anb[t % 3][:], atok[:, t, :], sc(ST_N, 2, t), None, ALU.mult, ALU.bypass,
           [("atok", t, hq) for hq in range(8)] + [("st", "rsN")], [("anb", t % 3)])

    def n2(t):
        bank = t % 2
        pv = psb(bank).bitcast(BF16)
        for c in range(4):
            tr(pv[:, c * 128:(c + 1) * 128], anb[t % 3][:, c * 128:(c + 1) * 128], [("anb", t % 3), "ident"], psk(bank))
        src = pv[:, 0:512].rearrange("p (c n) -> p c n", c=4)
        if t % 2:
            cp("dve", attnT[:, :, t * 128:(t + 1) * 128], src, psk(bank), [("attnT", t)])
        else:
            act(attnT[:, :, t * 128:(t + 1) * 128], src, AF.Copy, psk(bank), [("attnT", t)])

    drain(run_skewed([n1, n2], NT))
    dump("attnN", attnT[:], [("attnT", t) for t in range(NT)])
    pop(L, 3)
    pop(L, N_C)
    pop(L, 5)

    h2T = push(L, "h2T", [128, 8, S], BF16, ["h2T"])
    wdn = push(L, "wdn", [128, NCP, 1024], BF16, ["wdn"])
    wu = [push(L, "wu%d" % i, [128, 8, 256], BF16, ["wu"]) for i in range(3)]

    def wu_load(s_):
        P.dma("pool", wu[s_ % 3][:], wup_d[s_ % NCP], writes=[("wu", s_ % 3)], slot=("wu", s_ % 3))
    NXE, NYE, NHE = 5, 2, 3
    xe = [push(R, "xe%d" % i, [128, D], F32, ["xe"]) for i in range(NXE)]
    ye = [push(R, "ye%d" % i, [128, D], F32, ["ye"]) for i in range(NYE)]
    h2b = [push(R, "h2b%d" % i, [128, D], BF16, ["h2b"]) for i in range(NHE)]
    N_E = NXE + NYE + NHE

    for pc in range(4):
        k0, k1 = [0, 6, 12, 17][pc], [6, 12, 17, 22][pc]
        P.dma("pool", wdn[:, k0:k1, :], wdn_d[:, k0:k1, :], writes=[("wdn", pc)], slot=("wdn", pc))
    WDN = [("wdn", pc) for pc in range(4)]
    wu_load(0)
    wu_load(1)

    def e1(t):
        yb = 2 * (t % 3)
        tsl = slice(t * 128, (t + 1) * 128)
        if t == 0:
            for t2 in range(2):
                P.dma("sp", xe[t2 % NXE][:], x_d[t2 * 128:(t2 + 1) * 128, :], writes=[("xe", t2 % NXE)],
                      slot=("xe", t2 % NXE))
        if t + 2 < NT:
            t2 = t + 2
            P.dma("sp", xe[t2 % NXE][:], x_d[t2 * 128:(t2 + 1) * 128, :], writes=[("xe", t2 % NXE)],
                  slot=("xe", t2 % NXE))
        for dh in range(2):
            o = psb(yb + dh)
            for c in range(8):
                src = attnT if c < 4 else lruT
                keys = [("attnT", t)] if c < 4 else [("lruT", c - 4)]
                mm(o, src[:, c % 4, tsl], wob[:, c, dh * 512:(dh + 1) * 512], c == 0, c == 7,
                   keys + [("wob", c)], psk(yb + dh))
        jt, jkey = jk()
        act(jt, psb(yb, 2), AF.Square, psk(yb, 2), [("st", "e_ss", t), jkey], accum=sc(ST2, 0, t))

    def e2(t):
        yb = 2 * (t % 3)
        tsl = slice(t * 128, (t + 1) * 128)
        xb, yy = xe[t % NXE], ye[t % NYE]
        XK = ("xe", t % NXE)
        act(sc(ST2, 1, t), sc(ST2, 0, t), AF.Sqrt, [("st", "e_ss", t)], [("st", "e_sd", t)], scale=1.0 / D, bias=EPS)
        recip(sc(ST2, 2, t), sc(ST2, 1, t), [("st", "e_sd", t)], [("st", "e_rs", t)])
        stt(yy[:], psb(yb, 2), sc(ST2, 2, t), g2[:, 1, :], ALU.mult, ALU.mult,
            psk(yb, 2) + [("st", "e_rs", t), ("g2", 1)], [("ye", t % NYE)])
        tt("dve", xb[:], xb[:], yy[:], ALU.add, [XK, ("ye", t % NYE)], [XK])
        P.dma("sp", out_d[tsl, :], xb[:], reads=[XK], writes=[("out", t)], slot=("x1o", t % NXE))
        if debug:
            if t == 0:
                dbgx1.append(nc.dram_tensor("dbg_x1", [S, D], F32, kind="ExternalOutput").ap())
            P.dma("sp", dbgx1[0][tsl, :], xb[:], reads=[XK], slot=("dbgx1", t % NXE))

    def e3(t):
        xb, hb2 = xe[t % NXE], h2b[t % NHE]
        XK = ("xe", t % NXE)
        jt, jkey = jk()
        act(jt, xb[:], AF.Square, [XK], [("st", "f_ss", t), jkey], accum=sc(ST2, 3, t))
        act(sc(ST2, 4, t), sc(ST2, 3, t), AF.Sqrt, [("st", "f_ss", t)], [("st", "f_sd", t)], scale=1.0 / D, bias=EPS)
        recip(sc(ST2, 5, t), sc(ST2, 4, t), [("st", "f_sd", t)], [("st", "f_rs", t)])
        stt(hb2[:], xb[:], sc(ST2, 5, t), g2[:, 0, :], ALU.mult, ALU.mult,
            [XK, ("st", "f_rs", t), ("g2", 0)], [("h2b", t % NHE)])

    def e4(t):
        hb2 = h2b[t % NHE]
        bank = 6 + t % 2
        pv = psb(bank).bitcast(BF16)
        for c in range(8):
            tr(pv[:, c * 128:(c + 1) * 128], hb2[:, c * 128:(c + 1) * 128], [("h2b", t % NHE), "ident"], psk(bank))
        src = pv.rearrange("p (c n) -> p c n", c=8)
        if t % 2:
            act(h2T[:, :, t * 128:(t + 1) * 128], src, AF.Copy, psk(bank), [("h2T", t)])
        else:
            cp("dve", h2T[:, :, t * 128:(t + 1) * 128], src, psk(bank), [("h2T", t)])

    dbgx1 = []
    drain(run_skewed([e1, e2, e3, e4], NT))
    P.dma("sp", g2[:, 1, :], gbc_d[:, 3, :], writes=[("g2", 1)], slot=("g2", 1))
    dump("h2T", h2T[:], [("h2T", t) for t in range(NT)])
    pop(R, N_E)
    pop(R, 5)

    uT = push(L, "uT", [128, NCP, 1024], BF16, ["uT"])
    tg = [push(L, "tg%d" % i, [128, 1024], F32, ["tgv"]) for i in range(2)]
    tv = [push(L, "tv%d" % i, [128, 1024], F32, ["tgv"]) for i in range(2)]
    gg = [push(L, "gg%d" % i, [128, 1024], F32, ["gg"]) for i in range(2)]
    xf = [push(R, "xf%d" % i, [128, D], F32, ["xf"]) for i in range(4)]
    yf = [push(R, "yf%d" % i, [128, D], F32, ["yf"]) for i in range(2)]

    def fw(ch, k):
        return pp[:, PP_FW + ch * 3 + k:PP_FW + ch * 3 + k + 1]

    def fb(ch):
        return pp[:, PP_FB + ch:PP_FB + ch + 1]

    step = 0

    for hf in range(2):
        for cpi in range(NCP):
            w = wu[step % 3]
            if step + 2 < 2 * NCP:
                wu_load(step + 2)
            base = 4 * (step % 2)
            for gv in range(2):
                for tbl in range(2):
                    b = base + 2 * gv + tbl
                    tb = 2 * hf + tbl
                    for c in range(8):
                        mm(psb(b), w[:, c, gv * 128:(gv + 1) * 128], h2T[:, c, tb * 512:(tb + 1) * 512],
                           c == 0, c == 7, [("h2T", 4 * tb + j) for j in range(4)] + [("wu", step % 3)], psk(b))
            bufs = (tg[step % 2], tv[step % 2])
            for gv in range(2):
                ch = cpi + NCP * gv
                G = psb(base + 2 * gv, 2)
                T = bufs[gv]
                kp = psk(base + 2 * gv, 2)
                tk = ("tgv", gv, step % 2)
                act(T[:], G, AF.Identity, kp + ["pp"], [tk], scale=fw(ch, 2), bias=fb(ch))
                if hf == 0:
                    act(fprev[:, ch, :], G[:, 1022:1024], AF.Copy, kp, [("fprev", ch)])
                stt(T[:, 1:1024], G[:, 0:1023], fw(ch, 1), T[:, 1:1024], ALU.mult, ALU.add, kp + [tk, "pp"], [tk])
                stt(T[:, 2:1024], G[:, 0:1022], fw(ch, 0), T[:, 2:1024], ALU.mult, ALU.add, kp + [tk, "pp"], [tk])
                if hf == 1:
                    stt(T[:, 0:2], fprev[:, ch, 0:2], fw(ch, 0), T[:, 0:2], ALU.mult, ALU.add,
                        [("fprev", ch), tk, "pp"], [tk])
                    stt(T[:, 0:1], fprev[:, ch, 1:2], fw(ch, 1), T[:, 0:1], ALU.mult, ALU.add,
                        [("fprev", ch), tk, "pp"], [tk])
            g_ = gg[step % 2]
            act(g_[:], bufs[0][:], AF.Gelu_apprx_tanh, [("tgv", 0, step % 2)], [("gg", step % 2)])
            tt("pool", uT[:, cpi, :], g_[:], bufs[1][:], ALU.mult, [("gg", step % 2), ("tgv", 1, step % 2)],
               [("uT", cpi)])
            step += 1
        for tl in range(8):
            t = hf * 8 + tl
            tsl = slice(t * 128, (t + 1) * 128)
            yb = 2 * (t % 4)
            xb, yy = xf[t % 4], yf[t % 2]
            XK = ("xf", t % 4)
            P.dma("sp", xb[:], out_d[tsl, :], reads=[("out", t)], writes=[XK], slot=("xf", t % 4))
            for dh in range(2):
                o = psb(yb + dh)
                for k in range(NCP):
                    mm(o, uT[:, k, tl * 128:(tl + 1) * 128], wdn[:, k, dh * 512:(dh + 1) * 512],
                       k == 0, k == NCP - 1, [("uT", k)] + WDN, psk(yb + dh))
            Y = psb(yb, 2)
            jt, jkey = jk()
            act(jt, Y, AF.Square, psk(yb, 2), [("st", "g_ss", t), jkey], accum=sc(ST3, 0, t))
            act(sc(ST3, 1, t), sc(ST3, 0, t), AF.Sqrt, [("st", "g_ss", t)], [("st", "g_sd", t)], scale=1.0 / D, bias=EPS)
            recip(sc(ST3, 2, t), sc(ST3, 1, t), [("st", "g_sd", t)], [("st", "g_rs", t)])
            stt(yy[:], Y, sc(ST3, 2, t), g2[:, 1, :], ALU.mult, ALU.mult,
                psk(yb, 2) + [("st", "g_rs", t), ("g2", 1)], [("yf", t % 2)])
            tt("dve", xb[:], xb[:], yy[:], ALU.add, [XK, ("yf", t % 2)], [XK])
            P.dma("sp", out_d[tsl, :], xb[:], reads=[XK], writes=[("out", t)], slot=("xoo", t % 4))

    dump("uT", uT[:], [("uT", k) for k in range(NCP)])
    P.emit()
    psc.__exit__(None, None, None)
    return nc


def _host_layout(inp):
    f = np.float32
    w_in = np.asarray(inp["w_in"][0], f)
    cols = []
    for fc in range(4):
        cols.append(np.arange(fc * 128, (fc + 1) * 128))
    for h in range(2):
        r = 512 + h * 64 + np.arange(64)
        cols.append(np.concatenate([r, r]))
    cols.append(640 + np.arange(128))
    cols.append(640 + np.arange(128))
    for c in range(4):
        cols.append(768 + c * 128 + np.arange(128))
    for c in range(4):
        cols.append(1280 + c * 128 + np.arange(128))
    w_in_r = np.stack([w_in[:, cc].reshape(8, 128, 128).transpose(1, 0, 2) for cc in cols]).astype(f)
    w_out_r = np.ascontiguousarray(np.asarray(inp["w_out"][0], f).reshape(8, 128, 1024).transpose(1, 0, 2))
    w_up = np.asarray(inp["w_up"][0], f)
    w_up_r = np.empty((NCP, 128, 8, 256), f)
    for cpi in range(NCP):
        g = w_up[:, cpi * 128:(cpi + 1) * 128].reshape(8, 128, 128).transpose(1, 0, 2)
        v = w_up[:, DFF + cpi * 128:DFF + (cpi + 1) * 128].reshape(8, 128, 128).transpose(1, 0, 2)
        w_up_r[cpi, :, :, 0:128] = g
        w_up_r[cpi, :, :, 128:256] = v
    w_down_r = np.ascontiguousarray(np.asarray(inp["w_down"][0], f).reshape(NCP, 128, 1024).transpose(1, 0, 2))

    def pc(v, n):
        return np.asarray(v, f).reshape(n, 128).T

    pp = np.zeros((128, NPP), f)
    cwt = np.asarray(inp["lru_conv_w"][0], f)
    for c in range(4):
        for k in range(4):
            pp[:, c * 4 + k] = cwt[k, c * 128:(c + 1) * 128]
    pp[:, 16:20] = pc(inp["lru_conv_b"][0], 4)
    pp[:, 20:24] = pc(inp["lru_ba"][0], 4)
    pp[:, 24:28] = pc(inp["lru_bx"][0], 4)
    pp[:, 28:32] = pc(inp["lru_lambda"][0], 4)
    fwt = np.asarray(inp["ffn_conv_w"][0], f)
    for ch in range(44):
        for k in range(3):
            pp[:, 32 + ch * 3 + k] = fwt[k, ch * 128:(ch + 1) * 128]
    pp[:, 164:208] = pc(inp["ffn_conv_b"][0], 44)
    pp[:, 208:212] = pc(inp["norm_attn_out"][0], 4)
    pp[:, 212:216] = pc(inp["norm_lru_out"][0], 4)

    gbc = np.empty((128, 4, 1024), f)
    for i, k in enumerate(["norm_mix_pre", "norm_mix_post", "norm_ffn_pre", "norm_ffn_post"]):
        gbc[:, i, :] = np.asarray(inp[k][0], f)[None, :]

    btab = np.empty((128, 2, 2, 4, 128), f)
    kk = np.arange(128)[:, None]
    qq = np.arange(128)[None, :]
    for h in range(2):
        for rt in range(2):
            for slot in range(4):
                j, prev = slot % 2, slot // 2
                hq = 4 * h + 2 * j + rt
                slope = 2.0 ** (-(hq + 1))
                dist = (qq - kk) + (128 if prev else 0)
                valid = (dist >= 0) & (dist < 128)
                btab[:, h, rt, slot, :] = np.where(valid, -slope * dist, -30000.0)
    sinks = np.asarray(inp["sinks"][0], f)
    sinkbc = np.ascontiguousarray(np.broadcast_to(sinks[None, :], (128, 8))).astype(f)

    def bd(w):
        w = np.asarray(w[0], f)
        o = np.zeros((128, 4, 128), f)
        for c in range(4):
            for hh in range(2):
                o[hh * 64:(hh + 1) * 64, c, hh * 64:(hh + 1) * 64] = w[2 * c + hh]
        return o

    shared = {
        "w_in_r": w_in_r, "w_out_r": w_out_r, "w_up_r": w_up_r, "w_down_r": w_down_r,
        "pp": pp, "gbc": gbc, "btab": btab.reshape(128, 2048), "sinkbc": sinkbc,
        "wab": bd(inp["lru_wa"]), "wxb": bd(inp["lru_wx"]), "ident": np.eye(128, dtype=f),
    }
    return shared


_NC_CACHE = {}


def kernel(**inputs):
    x = np.asarray(inputs["x"], np.float32)
    shared = _host_layout(inputs)
    if "nc" not in _NC_CACHE:
        _NC_CACHE["nc"] = build_nc()
    nc = _NC_CACHE["nc"]
    in_maps = []
    for b in range(8):
        m = dict(shared)
        m["x"] = np.ascontiguousarray(x[b])
        in_maps.append(m)
    res = run_bass_kernel_spmd(nc, in_maps, core_ids=list(range(8)))
    return np.stack([np.asarray(r["out"], np.float32) for r in res.results], axis=0)
```

```python
import bisect
import contextlib
import numpy as np
import concourse.bass as bass
import concourse.mybir as mybir
from concourse.bass_utils import run_bass_kernel_spmd

F32 = mybir.dt.float32
BF16 = mybir.dt.bfloat16
AF = mybir.ActivationFunctionType
ALU = mybir.AluOpType

S = 2048
D = 1024
NT = 16
NB = 4
DFF = 2816
NCP = 22
EPS = 1e-6
NPP = 216

ENGS = ("pe", "act", "dve", "pool", "sp")


class Op:
    __slots__ = ("eng", "fn", "reads", "writes", "deps", "idx", "is_dma", "slot", "sig", "val")

    def __init__(self, eng, fn, reads, writes, is_dma, slot):
        self.eng, self.fn, self.reads, self.writes = eng, fn, reads, writes
        self.is_dma, self.slot = is_dma, slot
        self.deps = set()
        self.sig = False
        self.val = None


class Prog:
    def __init__(self, nc):
        self.nc = nc
        self.ops = []
        self.last_writer = {}
        self.readers = {}
        self.pending = {}

    @staticmethod
    def fam(k):
        return k if isinstance(k, str) else k[0]

    def frontier(self, fams):
        fams = set(fams)
        out = set()
        for k, j in self.last_writer.items():
            if self.fam(k) in fams:
                out.add(j)
        for k, js in self.readers.items():
            if self.fam(k) in fams:
                out.update(js)
        return out

    def op(self, eng, fn, reads=(), writes=(), dma=False, slot=None):
        o = Op(eng, fn, tuple(reads), tuple(writes), dma, slot)
        o.idx = len(self.ops)
        lw, rd = self.last_writer, self.readers
        if self.pending:
            for k in o.reads + o.writes:
                pd = self.pending.get(self.fam(k))
                if pd:
                    o.deps.update(pd)
        for k in o.reads:
            j = lw.get(k)
            if j is not None:
                o.deps.add(j)
            if self.fam(k) == "ps":
                for r in rd.get(k, ()):
                    if self.ops[r].eng != eng:
                        o.deps.add(r)
        for k in o.writes:
            j = lw.get(k)
            if j is not None:
                o.deps.add(j)
            for r in rd.get(k, ()):
                o.deps.add(r)
        for k in o.reads:
            rd.setdefault(k, []).append(o.idx)
        for k in o.writes:
            lw[k] = o.idx
            rd[k] = []
        o.deps.discard(o.idx)
        self.ops.append(o)
        return o

    def dma(self, eng, out, in_, reads=(), writes=(), slot=None):
        return self.op(eng, lambda e: e.dma_start(out=out, in_=in_), reads, writes,
                       dma=True, slot=slot)

    def emit(self):
        nc = self.nc
        ops = self.ops
        for o in ops:
            for j in o.deps:
                p = ops[j]
                if p.eng == "pe" and o.eng == "pe" and not p.is_dma:
                    continue
                p.sig = True
        slots = sorted({o.slot for o in ops if o.is_dma}, key=str)
        st = contextlib.ExitStack()
        with st:
            esem = {e: st.enter_context(nc.semaphore("s_" + e)) for e in ENGS}
            ssem = {s: st.enter_context(nc.semaphore("d_%s" % (str(s),))) for s in slots}
            ecnt = {e: 0 for e in ENGS}
            scnt = {s: 0 for s in slots}
            slot_hist = {s: [] for s in slots}
            for o in ops:
                if o.is_dma:
                    scnt[o.slot] += 16
                    o.val = scnt[o.slot]
                    slot_hist[o.slot].append((o.idx, o.val))
                elif o.sig:
                    ecnt[o.eng] += 1
                    o.val = ecnt[o.eng]
            per_eng = {e: [o for o in ops if o.eng == e] for e in ENGS}

            def slot_val_before(s, idx):
                h = slot_hist[s]
                k = bisect.bisect_left(h, (idx, -1))
                return h[k - 1][1] if k > 0 else 0

            block = st.enter_context(nc.Block())

            def make(e):
                def body(eng):
                    waited = {}
                    for o in per_eng[e]:
                        need = {}
                        for j in o.deps:
                            p = ops[j]
                            if p.is_dma:
                                key = ("s", p.slot)
                                v = slot_val_before(p.slot, o.idx)
                            else:
                                if p.eng == "pe" and o.eng == "pe":
                                    continue
                                key = ("e", p.eng)
                                v = p.val
                            if v > need.get(key, 0):
                                need[key] = v
                        for key, v in need.items():
                            if waited.get(key, 0) >= v:
                                continue
                            waited[key] = v
                            sem = ssem[key[1]] if key[0] == "s" else esem[key[1]]
                            eng.wait_ge(sem, v)
                        ins = o.fn(eng)
                        if o.is_dma:
                            ins.then_inc(ssem[o.slot], 16)
                        elif o.sig:
                            ins.then_inc(esem[o.eng], 1)
                    if e == "sp":
                        for s in slots:
                            if waited.get(("s", s), 0) < scnt[s]:
                                eng.wait_ge(ssem[s], scnt[s])
                return body

            block.tensor(make("pe"))
            block.scalar(make("act"))
            block.vector(make("dve"))
            block.gpsimd(make("pool"))
            block.sync(make("sp"))


def build_nc(debug=False):
    nc = bass.Bass("TRN2", target_bir_lowering=False)
    dbg_n = [0]

    def din(name, shape):
        return nc.dram_tensor(name, list(shape), F32, kind="ExternalInput").ap()

    x_d = din("x", [S, D])
    win_d = din("w_in_r", [16, 128, 8, 128])
    wout_d = din("w_out_r", [128, 8, 1024])
    wup_d = din("w_up_r", [NCP, 128, 8, 256])
    wdn_d = din("w_down_r", [128, NCP, 1024])
    pp_d = din("pp", [128, NPP])
    gbc_d = din("gbc", [128, 4, 1024])
    btab_d = din("btab", [128, 2048])
    sink_d = din("sinkbc", [128, 8])
    wa_d = din("wab", [128, 4, 128])
    wx_d = din("wxb", [128, 4, 128])
    id_d = din("ident", [128, 128])
    out_d = nc.dram_tensor("out", [S, D], F32, kind="ExternalOutput").ap()

    P = Prog(nc)

    stacks = {"left": [], "right": []}
    retired = {"left": set(), "right": set()}
    ptr = {"left": 16512, "right": 229376 - 64}

    def push(side, name, shape, dt, fams):
        nbytes = int(np.prod(shape[1:])) * (4 if dt == F32 else 2)
        nbytes = (nbytes + 63) // 64 * 64
        if side == "left":
            off = ptr["left"]
            ptr["left"] += nbytes
        else:
            ptr["right"] -= nbytes
            off = ptr["right"]
        assert ptr["left"] <= ptr["right"], ("SBUF overflow", name, ptr)
        t = nc.alloc_sbuf_tensor_at(name, list(shape), dt, offset=off)
        stacks[side].append((nbytes, tuple(fams)))
        allret = retired["left"] | retired["right"]
        if allret:
            fs = frozenset(allret)
            for f in fams:
                P.pending[f] = fs
        return t

    def pop(side, n):
        for _ in range(n):
            nbytes, fams = stacks[side].pop()
            retired[side] |= P.frontier(fams)
            if side == "left":
                ptr["left"] -= nbytes
            else:
                ptr["right"] += nbytes

    L, R = "left", "right"

    pp = push(L, "pp", [128, NPP], F32, ["pp"])
    g2 = push(L, "g2", [128, 2, 1024], F32, ["g2"])
    ident = push(L, "ident", [128, 128], BF16, ["ident"])
    ones = push(L, "ones", [128, 128], BF16, ["ones"])
    wab = push(L, "wab", [128, 4, 128], BF16, ["wab"])
    wxb = push(L, "wxb", [128, 4, 128], BF16, ["wxb"])
    dp = push(L, "dp", [128, 64], F32, ["dp"])
    junk2 = push(L, "junk", [128, 3, 1024], BF16, ["junk"])
    jcnt = [0]

    def jk(n_=1024):
        i = jcnt[0] % 3
        jcnt[0] += 1
        return junk2[:, i, 0:n_], ("junk", i)
    stats = push(L, "stats", [128, 256], F32, ["st"])
    fprev = push(L, "fprev", [128, 2 * NCP, 2], F32, ["fprev"])
    esink = push(L, "esink", [128, 8], F32, ["esink"])
    psc = nc.psum_tensor("ps", [128, 8, 512], F32)
    ps = psc.__enter__()

    def psb(b, n=1):
        return ps[:, b:b + n, :].rearrange("p b n -> p (b n)")

    def psk(b, n=1):
        return [("ps", b + i) for i in range(n)]

    def act(out, in_, func, reads, writes, scale=1.0, bias=0.0, accum=None):
        kw = dict(out=out, in_=in_, func=func, scale=scale, bias=bias)
        if accum is not None:
            kw["accum_out"] = accum
        return P.op("act", lambda e: e.activation(**kw), reads, writes)

    def tt(eng, out, in0, in1, op, reads, writes):
        return P.op(eng, lambda e: e.tensor_tensor(out=out, in0=in0, in1=in1, op=op), reads, writes)

    def stt(out, in0, scalar, in1, op0, op1, reads, writes):
        return P.op("dve", lambda e: e.scalar_tensor_tensor(out=out, in0=in0, scalar=scalar, in1=in1,
                                                            op0=op0, op1=op1), reads, writes)

    def ts(eng, out, in0, s1, s2, op0, op1, reads, writes):
        return P.op(eng, lambda e: e.tensor_scalar(out=out, in0=in0, scalar1=s1, scalar2=s2,
                                                   op0=op0, op1=op1), reads, writes)

    def cp(eng, out, in_, reads, writes):
        return P.op(eng, lambda e: e.tensor_copy(out=out, in_=in_), reads, writes)

    def mm(out, lhsT, rhs, start, stop, reads, writes):
        return P.op("pe", lambda e: e.matmul(out, lhsT=lhsT, rhs=rhs, start=start, stop=stop),
                    reads, writes)

    def tr(out, in_, reads, writes):
        return P.op("pe", lambda e: e.transpose(out, in_, ident[:]), reads, writes)

    def recip(out, in_, reads, writes):
        return P.op("dve", lambda e: e.reciprocal(out=out, in_=in_), reads, writes)

    def dump(name, t, keys):
        if not debug:
            return
        d = nc.dram_tensor("dbg_" + name, list(t.shape), t.dtype, kind="ExternalOutput").ap()
        dbg_n[0] += 1
        P.dma("sp", d, t, reads=keys, slot=("dbg", dbg_n[0]))

    P.dma("sp", pp[:], pp_d[:, :], writes=["pp"], slot="c0")
    P.dma("sp", g2[:, 0, :], gbc_d[:, 0, :], writes=[("g2", 0)], slot=("g2", 0))
    P.dma("sp", g2[:, 1, :], gbc_d[:, 1, :], writes=[("g2", 1)], slot=("g2", 1))
    P.dma("pool", ident[:], id_d[:, :], writes=["ident"], slot="c2")
    P.dma("pool", wab[:], wa_d[:, :, :], writes=["wab"], slot="c3")
    P.dma("pool", wxb[:], wx_d[:, :, :], writes=["wxb"], slot="c4")
    P.dma("sp", esink[:], sink_d[:, :], writes=["esink"], slot="c6")
    P.op("dve", lambda e: e.memset(ones[:], 1.0), writes=["ones"])

    def cw(c, k):
        return pp[:, c * 4 + k:c * 4 + k + 1]
    PP_CB, PP_BA, PP_BX, PP_LAM, PP_FW, PP_FB, PP_GM = 16, 20, 24, 28, 32, 164, 208
    DP_CV, DP_CH, DP_HBA, DP_HBX = 0, 4, 8, 12

    lam = pp[:, PP_LAM:PP_LAM + 4]
    t_ax, t_e, t_l, t_s, t_m, t_mx, t_nl, t_c = (dp[:, 16 + 4 * i:20 + 4 * i] for i in range(8))
    P.op("dve", lambda e: e.memset(t_c, 0.1), writes=["d_c"])
    ts("dve", t_nl, lam, -1.0, None, ALU.mult, ALU.bypass, ["pp"], ["d_nl"])
    tt("dve", t_ax, lam, t_nl, ALU.max, ["pp", "d_nl"], ["d_ax"])
    act(t_e, t_ax, AF.Exp, ["d_ax"], ["d_e"], scale=-1.0)
    act(t_l, t_e, AF.Ln, ["d_e"], ["d_l"], bias=1.0)
    act(esink[:], esink[:], AF.Exp, ["esink"], ["esink"])
    ts("dve", t_s, t_e, -1.0 / 3.0, 0.5, ALU.mult, ALU.add, ["d_e"], ["d_s"])
    tt("dve", t_s, t_s, t_e, ALU.mult, ["d_s", "d_e"], ["d_s"])
    ts("dve", t_s, t_s, -1.0, 1.0, ALU.mult, ALU.add, ["d_s"], ["d_s"])
    tt("dve", t_s, t_s, t_e, ALU.mult, ["d_s", "d_e"], ["d_s"])
    tt("dve", t_m, t_e, t_c, ALU.is_lt, ["d_e", "d_c"], ["d_m"])
    tt("dve", t_s, t_s, t_l, ALU.subtract, ["d_s", "d_l"], ["d_s"])
    tt("dve", t_s, t_s, t_m, ALU.mult, ["d_s", "d_m"], ["d_s"])
    tt("dve", t_l, t_l, t_s, ALU.add, ["d_s", "d_l"], ["d_l"])
    ts("dve", t_mx, t_nl, 0.0, None, ALU.max, ALU.bypass, ["d_nl"], ["d_mx"])
    tt("dve", t_l, t_l, t_mx, ALU.add, ["d_l", "d_mx"], ["d_l"])
    ts("dve", dp[:, DP_CV:DP_CV + 4], t_l, -8.0, None, ALU.mult, ALU.bypass, ["d_l"], ["d_cv"])
    ts("dve", dp[:, DP_CH:DP_CH + 4], t_l, -4.0, None, ALU.mult, ALU.bypass, ["d_l"], ["d_ch"])
    ts("dve", dp[:, DP_HBA:DP_HBA + 4], pp[:, PP_BA:PP_BA + 4], 0.5, None, ALU.mult, ALU.bypass, ["pp"], ["d_hba"])
    ts("dve", dp[:, DP_HBX:DP_HBX + 4], pp[:, PP_BX:PP_BX + 4], 0.5, None, ALU.mult, ALU.bypass, ["pp"], ["d_hbx"])
    DPK = ["d_cv", "d_ch", "d_hba", "d_hbx"]

    ST_A = 0
    ST2 = 48
    ST3 = 144
    ST_N = 192

    def sc(base, k, t):
        return stats[:, base + 16 * k + t:base + 16 * k + t + 1]

    def run_skewed(stages, n):
        ns = len(stages)
        for step in range(n + ns - 1):
            for k, stg in enumerate(stages):
                t = step - k
                if 0 <= t < n:
                    stg(t)
                    yield

    def drain(gen):
        for _ in gen:
            pass

    lruT = push(R, "lruT", [128, 4, S], BF16, ["lruT", "lruT0"])
    lxp = push(L, "lxp", [128, 4, S + 4], BF16, ["lx", "lxpad"])
    glT = push(L, "glT", [128, 4, S], BF16, ["gl"])
    qT = push(L, "qT", [128, 4, S], BF16, ["qT"])
    kT = push(L, "kT", [128, 2, S], BF16, ["kT"])
    vaug = push(L, "vaug", [128, NT, 2, 66], BF16, ["v", "vone"])
    dset = []
    sqb = None

    def D_ALLOC():
        nonlocal sqb
        for i in range(2):
            dset.append(dict(
                xc=push(R, "xc%d" % i, [128, S], F32, ["xc"]),
                xcb=push(R, "xcb%d" % i, [128, S // 2], BF16, ["xcb"]),
                tha=push(R, "tha%d" % i, [128, S], F32, ["tha"]),
                thx=push(R, "thx%d" % i, [128, S], F32, ["thx"]),
                a2=push(R, "a2%d" % i, [128, S], F32, ["a2"])))
        sqb = [push(R, "sqb%d" % i, [128, 512], BF16, ["sqb"]) for i in range(2)]
    N_D = 10 + 2
    hT = push(L, "hT", [128, 8, S], BF16, ["hT"])
    wc = [push(L, "wc%d" % i, [128, 8, 128], BF16, ["wc"]) for i in range(3)]
    wv = push(L, "wv", [128, 8, 128], BF16, ["wv"])
    N_B = 1 + 3 + 1
    NXT, NHB = 8, 4
    xt = [push(L, "xt%d" % i, [128, D], F32, ["xt"]) for i in range(NXT)]
    hb = [push(L, "hb%d" % i, [128, D], BF16, ["hb"]) for i in range(NHB)]
    N_A = NXT + NHB

    P.op("pool", lambda e: e.memset(lxp[:, :, 0:4], 0.0), writes=[("lxpad",)])
    P.op("pool", lambda e: e.memset(vaug[:, :, :, 64:66], 1.0), writes=[("vone",)])

    def a1(t):
        xb = xt[t % NXT]
        P.dma("sp", xb[:], x_d[t * 128:(t + 1) * 128, :], writes=[("xt", t % NXT)], slot=("xt", t % NXT))
        jt, jkey = jk()
        act(jt, xb[:], AF.Square, [("xt", t % NXT)], [("st", "ssA", t), jkey], accum=sc(ST_A, 0, t))

    def a2_(t):
        xb, hbb = xt[t % NXT], hb[t % NHB]
        act(sc(ST_A, 1, t), sc(ST_A, 0, t), AF.Sqrt, [("st", "ssA", t)], [("st", "sdA", t)], scale=1.0 / D, bias=EPS)
        recip(sc(ST_A, 2, t), sc(ST_A, 1, t), [("st", "sdA", t)], [("st", "rsA", t)])
        stt(hbb[:], xb[:], sc(ST_A, 2, t), g2[:, 0, :], ALU.mult, ALU.mult,
            [("xt", t % NXT), ("st", "rsA", t), ("g2", 0)], [("hb", t % NHB)])

    def a3(t):
        hbb = hb[t % NHB]
        bank = t % 2
        pv = psb(bank).bitcast(BF16)
        for c in range(8):
            tr(pv[:, c * 128:(c + 1) * 128], hbb[:, c * 128:(c + 1) * 128],
               [("hb", t % NHB), "ident"], psk(bank))
        cp("dve", hT[:, :, t * 128:(t + 1) * 128], pv.rearrange("p (c n) -> p c n", c=8), psk(bank), [("hT", t)])

    def hTk(tb):
        return [("hT", 4 * tb + j) for j in range(4)]

    bank_rr = [2, 3, 4, 5]
    bank_rr6 = [2, 3, 4, 5, 0, 1]
    use6 = [False]
    nbc = [0]
    evc = [0]

    def b_unit(w, wkey, fc, tb):
        b = (bank_rr6[nbc[0] % 6] if use6[0] else bank_rr[nbc[0] % 4]); nbc[0] += 1
        o = psb(b)
        sl = slice(tb * 512, (tb + 1) * 512)
        for c in range(8):
            mm(o, w[:, c, :], hT[:, c, sl], c == 0, c == 7, hTk(tb) + [wkey], psk(b))
        if fc < 4:
            act(qT[:, fc, sl], o, AF.Copy, psk(b), [("qT", fc, tb)], scale=0.125)
        elif fc < 6:
            act(kT[:, fc - 4, sl], o, AF.Copy, psk(b), [("kT", fc - 4, tb)])
        elif fc < 12:
            c4 = fc - 8
            cp("dve", lxp[:, c4, 4 + tb * 512:4 + (tb + 1) * 512], o, psk(b), [("lx", c4, tb)])
        else:
            c4 = fc - 12
            act(glT[:, c4, sl], o, AF.Gelu_apprx_tanh, psk(b), [("gl", c4, tb)])

    def v_unit(t0):
        b = (bank_rr6[nbc[0] % 6] if use6[0] else bank_rr[nbc[0] % 4]); nbc[0] += 1
        for j in range(4):
            t = t0 + j
            o = psb(b)[:, j * 128:(j + 1) * 128]
            for c in range(8):
                mm(o, hT[:, c, t * 128:(t + 1) * 128], wv[:, c, :], c == 0, c == 7, [("hT", t), "wv"], psk(b))
        dst = vaug[:, t0:t0 + 4, :, 0:64]
        src_ = psb(b).rearrange("p (t h m) -> p t h m", t=4, h=2)
        keys = [("v", t0 + j) for j in range(4)]
        if (t0 // 4) % 2:
            cp("dve", dst, src_, psk(b), keys)
        else:
            act(dst, src_, AF.Copy, psk(b), keys)

    order = [8, 9, 10, 11, 12, 13, 14, 15, 4, 5, 0, 1, 2, 3]
    P.dma("pool", wv[:], win_d[6], writes=["wv"], slot="wv")
    P.dma("pool", wc[0][:], win_d[order[0]], writes=[("wc", 0)], slot=("wc", 0))

    def b_gen():
        for n, fc in enumerate(order):
            w = wc[n % 3]
            if n > 0:
                P.dma("pool", w[:], win_d[fc], writes=[("wc", n % 3)], slot=("wc", n % 3))
            for tb in range(NB):
                b_unit(w, ("wc", n % 3), fc, tb)
                yield (n, tb)
            if n == 9:
                for t in range(0, NT, 4):
                    v_unit(t)
                    yield (n, 0)

    etab = ex = PT = dsm = atok = None

    def C_ALLOC():
        nonlocal etab, ex, PT, dsm, atok
        etab = push(L, "etab", [128, 2, 1024], BF16, ["etab"])
        ex = []
        PT = [push(L, "PT%d" % i, [128, 1024], BF16, ["PT"]) for i in range(2)]
        dsm = push(L, "dsm", [128, 2, 4], F32, ["dsm"])
        atok = push(L, "atok", [128, NT, 512], BF16, ["atok"])
        P.dma("pool", etab[:].rearrange("p h n -> p (h n)"), btab_d[:, :], writes=["etab"], slot="c5")
    N_C = 1 + 2 + 1 + 1

    def S_step(s):
        i, h = s // 2, s % 2
        base = 2 * (s % 2)
        tb = i // 4
        etv = etab[:, h, :].rearrange("p (r n) -> p r n", r=2)
        for rt in range(2):
            b = base + rt
            o = psb(b).rearrange("p (s q) -> p s q", s=4)
            rows = slice(rt * 64, (rt + 1) * 64)
            mm(psb(b), ident[:], etv[:, rt, :], True, False, ["ident", "etab"], psk(b))
            for j in range(2):
                qr = qT[rows, 2 * h + j, i * 128:(i + 1) * 128]
                last = (j == 1)
                mm(o[:, j, :], kT[rows, h, i * 128:(i + 1) * 128], qr, False, last and i == 0,
                   [("kT", h, tb), ("qT", 2 * h + j, tb)], psk(b))
                if i > 0:
                    mm(o[:, 2 + j, :], kT[rows, h, (i - 1) * 128:i * 128], qr, False, last,
                       [("kT", h, (i - 1) // 4), ("qT", 2 * h + j, tb)], psk(b))

    def view4(a):
        return a.rearrange("p (r s q) -> p r s q", r=2, s=4)

    def E_step(s):
        base = 2 * (s % 2)
        act(PT[s % 2][:], psb(base, 2), AF.Exp, psk(base, 2), [("PT", s % 2)])

    def PV_step(s):
        i, h = s // 2, s % 2
        ptv = view4(PT[s % 2][:])
        ob = 4 + s % 2
        o4 = psb(ob).rearrange("p (g n) -> p g n", g=4)
        for g in range(4):
            rt, j = g % 2, g // 2
            mm(o4[:, g, 0:65], ptv[:, rt, j, :], vaug[:, i, h, 0:65], True, i == 0,
               [("PT", s % 2), ("v", i), ("vone",)], psk(ob))
            if i > 0:
                mm(o4[:, g, 0:65], ptv[:, rt, 2 + j, :], vaug[:, i - 1, h, 0:65], False, True,
                   [("PT", s % 2), ("v", i - 1), ("vone",)], psk(ob))
        d = dsm[:, s % 2, :]
        tt("dve", d, o4[:, :, 64], esink[:, 4 * h:4 * h + 4], ALU.add, psk(ob) + ["esink"], [("dsm", s % 2)])
        recip(d, d, [("dsm", s % 2)], [("dsm", s % 2)])
        for g in range(4):
            hq = 4 * h + g
            ts("dve", atok[:, i, hq * 64:(hq + 1) * 64], o4[:, g, 0:64], dsm[:, s % 2, g:g + 1], None,
               ALU.mult, ALU.bypass, psk(ob) + [("dsm", s % 2)], [("atok", i, hq)])

    def c_gen():
        NS = 2 * NT
        S_step(0)
        for s in range(NS):
            if s + 1 < NS:
                S_step(s + 1)
            E_step(s)
            yield
            PV_step(s)
            if s % 2 == 1:
                t_ = s // 2
                jt, jkey = jk(512)
                act(jt, atok[:, t_, :], AF.Square, [("atok", t_, hq) for hq in range(8)],
                    [("st", "ssN", t_), jkey], accum=sc(ST_N, 0, t_))
            yield

    def chunk_gen(c, si):
        H2 = S // 2
        B_ = dset[si]
        xc, xcb, tha, thx, a2 = B_["xc"], B_["xcb"], B_["tha"], B_["thx"], B_["a2"]
        hs = [slice(0, H2), slice(H2, S)]
        LX = [("lx", c, tb) for tb in range(NB)] + [("lxpad",)]
        for hf in range(2):
            ts("dve", xc[:, hs[hf]], lxp[:, c, 4 + hf * H2:4 + (hf + 1) * H2], cw(c, 3),
               pp[:, PP_CB + c:PP_CB + c + 1], ALU.mult, ALU.add, LX + ["pp"], [("xc", si, hf)])
            yield
        for k in (2, 1, 0):
            for hf in range(2):
                stt(xc[:, hs[hf]], lxp[:, c, 1 + k + hf * H2:1 + k + (hf + 1) * H2], cw(c, k), xc[:, hs[hf]],
                    ALU.mult, ALU.add, LX + ["pp", ("xc", si, hf)], [("xc", si, hf)])
                yield
        for tb in range(NB):
            sl = slice(tb * 512, (tb + 1) * 512)
            if tb % 2 == 0:
                cp("dve", xcb[:], xc[:, hs[tb // 2]], [("xc", si, tb // 2)], [("xcb", si)])
                yield
            sl2 = slice((tb % 2) * 512, (tb % 2 + 1) * 512)
            mm(psb(6), wab[:, c, :], xcb[:, sl2], True, True, [("xcb", si), "wab"], psk(6))
            mm(psb(7), wxb[:, c, :], xcb[:, sl2], True, True, [("xcb", si), "wxb"], psk(7))
            act(tha[:, sl], psb(6), AF.Tanh, psk(6) + DPK, [("tha", si, tb)], scale=0.5,
                bias=dp[:, DP_HBA + c:DP_HBA + c + 1])
            act(thx[:, sl], psb(7), AF.Tanh, psk(7) + DPK, [("thx", si, tb)], scale=0.5,
                bias=dp[:, DP_HBX + c:DP_HBX + c + 1])
            yield
        for hf in range(2):
            hk = [("tha", si, 2 * hf), ("tha", si, 2 * hf + 1)]
            act(a2[:, hs[hf]], tha[:, hs[hf]], AF.Exp, hk + DPK, [("a2", si, hf)], scale=dp[:, DP_CV + c:DP_CV + c + 1],
                bias=dp[:, DP_CV + c:DP_CV + c + 1])
            act(tha[:, hs[hf]], tha[:, hs[hf]], AF.Exp, hk + DPK, hk, scale=dp[:, DP_CH + c:DP_CH + c + 1],
                bias=dp[:, DP_CH + c:DP_CH + c + 1])
            yield
            xk = [("thx", si, 2 * hf), ("thx", si, 2 * hf + 1)]
            stt(thx[:, hs[hf]], thx[:, hs[hf]], 1.0, xc[:, hs[hf]], ALU.add, ALU.mult, xk + [("xc", si, hf)], xk)
            yield
        A2 = [("a2", si, 0), ("a2", si, 1)]
        act(a2[:], a2[:], AF.Sqrt, A2, A2, scale=-1.0, bias=1.0)
        yield
        for hf in range(2):
            xk = [("thx", si, 2 * hf), ("thx", si, 2 * hf + 1)]
            stt(thx[:, hs[hf]], thx[:, hs[hf]], 0.5, a2[:, hs[hf]], ALU.mult, ALU.mult, xk + [("a2", si, hf)], xk)
            yield
        for hf in range(2):
            hk = [("tha", si, 2 * hf), ("tha", si, 2 * hf + 1)]
            xk = [("thx", si, 2 * hf), ("thx", si, 2 * hf + 1)]
            init = 0.0 if hf == 0 else a2[:, H2 - 1:H2]
            P.op("dve", lambda e, hf=hf, init=init: e.tensor_tensor_scan(
                out=a2[:, hs[hf]], data0=tha[:, hs[hf]], data1=thx[:, hs[hf]], initial=init,
                op0=ALU.mult, op1=ALU.add), hk + xk + ([("a2", si, 0)] if hf else []), [("a2", si, hf)])
            yield
        for hf in range(2):
            tt("dve", lruT[:, c, hs[hf]], a2[:, hs[hf]], glT[:, c, hs[hf]], ALU.mult,
               [("a2", si, hf)] + [("gl", c, tb) for tb in range(NB)], [("lruT", c)] if hf else [("lruT0", c)])
            yield

    def d_gen():
        def chain2(c0, c1, si):
            for g_ in (chunk_gen(c0, si), chunk_gen(c1, si)):
                for _ in g_:
                    yield
        gens = [chain2(0, 2, 0), chain2(1, 3, 1)]
        alive = [True, True]
        for _ in range(12):
            next(gens[0])
            yield
        while alive[0] or alive[1]:
            for gi in range(2):
                if alive[gi]:
                    try:
                        next(gens[gi])
                        yield
                    except StopIteration:
                        alive[gi] = False
        xc = dset[0]["xc"]
        rbc = xc
        n = 0
        for tb in range(NB):
            sl = slice(tb * 512, (tb + 1) * 512)
            bank = 6 + tb % 2
            for c in range(4):
                sb_ = sqb[n % 2]
                tt("pool", sb_[:], lruT[:, c, sl], lruT[:, c, sl], ALU.mult, [("lruT", c), ("lruT0", c)], [("sqb", n % 2)])
                mm(psb(bank), ones[:], sb_[:], c == 0, c == 3, [("sqb", n % 2), "ones"], psk(bank))
                n += 1
                yield
            act(rbc[:, sl], psb(bank), AF.Ln, psk(bank), [("xc", 0, tb // 2)], scale=1.0 / 512, bias=EPS)
            yield
        act(rbc[:], rbc[:], AF.Exp, [("xc", 0, 0), ("xc", 0, 1)], [("xc", 0, 0), ("xc", 0, 1)], scale=-0.5)
        yield
        for c in range(4):
            tt("dve", lruT[:, c, :], lruT[:, c, :], rbc[:], ALU.mult,
               [("xc", 0, 0), ("xc", 0, 1), ("lruT", c), ("lruT0", c)], [("lruT", c), ("lruT0", c)])
            yield

    def step_gen(g, n=1):
        for _ in range(n):
            try:
                next(g)
            except StopIteration:
                return False
        return True

    gb = b_gen()
    gd = d_gen()
    ns = 3
    for stepi in range(NT + ns - 1):
        for k, stg in enumerate([a1, a2_, a3]):
            t = stepi - k
            if 0 <= t < NT:
                stg(t)
                if k == 2 and t % 4 == 3:
                    next(gb)
    pop(L, N_A)
    D_ALLOC()
    use6[0] = True
    d_alive = True
    vcnt = 0
    for (n, tb) in gb:
        if n >= 8 and d_alive:
            if tb == 4:
                vcnt += 1
                if vcnt % 2 == 0:
                    d_alive = step_gen(gd, 1)
            else:
                d_alive = step_gen(gd, 2)
    P.dma("sp", g2[:, 0, :], gbc_d[:, 2, :], writes=[("g2", 0)], slot=("g2", 0))
    dump("hT", hT[:], [("hT", t) for t in range(NT)])
    dump("qT", qT[:], [("qT", a, b) for a in range(4) for b in range(NB)])
    dump("kT", kT[:], [("kT", a, b) for a in range(2) for b in range(NB)])
    dump("lxp", lxp[:], [("lx", a, b) for a in range(4) for b in range(NB)] + [("lxpad",)])
    dump("glT", glT[:], [("gl", a, b) for a in range(4) for b in range(NB)])
    pop(L, N_B)
    C_ALLOC()
    gc = c_gen()
    c_alive = True
    while c_alive or d_alive:
        if c_alive:
            c_alive = step_gen(gc)
        if d_alive:
            d_alive = step_gen(gd, 2)
    ATOK = [("atok", i, hq) for i in range(NT) for hq in range(8)]
    dump("lruN", lruT[:], [("lruT", c) for c in range(4)])
    dump("xc3", dset[1]["xc"][:], [("xc", 1, 0), ("xc", 1, 1)])
    dump("tha3", dset[1]["tha"][:], [("tha", 1, tb) for tb in range(NB)])
    dump("u3", dset[1]["thx"][:], [("thx", 1, tb) for tb in range(NB)])
    dump("h3", dset[1]["a2"][:], [("a2", 1, 0), ("a2", 1, 1)])
    dump("atok", atok[:], ATOK)
    pop(R, N_D)

    attnT = push(R, "attnT", [128, 4, S], BF16, ["attnT"])
    wo32 = [push(R, "wo32_%d" % i, [128, 2, 1024], F32, ["wo32"]) for i in range(2)]
    wob = push(R, "wob", [128, 8, 1024], BF16, ["wob"])
    for i in range(4):
        P.dma("sp", wo32[i % 2][:], wout_d[:, 2 * i:2 * i + 2, :], writes=[("wo32", i % 2)], slot=("wo32", i % 2))
        for cc in range(2):
            c = 2 * i + cc
            ts("pool", wob[:, c, :], wo32[i % 2][:, cc, :], pp[:, PP_GM + c:PP_GM + c + 1], 1.0, ALU.mult, ALU.mult,
               [("wo32", i % 2), "pp"], [("wob", c)])
    anb = [push(L, "anb%d" % i, [128, 512], BF16, ["anb"]) for i in range(3)]
    SSN = [("st", "ssN", t) for t in range(NT)]
    act(stats[:, ST_N + 16:ST_N + 32], stats[:, ST_N:ST_N + 16], AF.Sqrt, SSN, [("st", "sdN")], scale=1.0 / 512, bias=EPS)
    recip(stats[:, ST_N + 32:ST_N + 48], stats[:, ST_N + 16:ST_N + 32], [("st", "sdN")], [("st", "rsN")])

    def n1(t):
        ts("dve", anb[t % 3][:], atok[:, t, :], sc(ST_N, 2, t), None, ALU.mult, ALU.bypass,
           [("atok", t, hq) for hq in range(8)] + [("st", "rsN")], [("anb", t % 3)])

    def n2(t):
        bank = t % 2
        pv = psb(bank).bitcast(BF16)
        for c in range(4):
            tr(pv[:, c * 128:(c + 1) * 128], anb[t % 3][:, c * 128:(c + 1) * 128], [("anb", t % 3), "ident"], psk(bank))
        src = pv[:, 0:512].rearrange("p (c n) -> p c n", c=4)
        if t % 2:
            cp("dve", attnT[:, :, t * 128:(t + 1) * 128], src, psk(bank), [("attnT", t)])
        else:
            act(attnT[:, :, t * 128:(t + 1) * 128], src, AF.Copy, psk(bank), [("attnT", t)])

    drain(run_skewed([n1, n2], NT))
    dump("attnN", attnT[:], [("attnT", t) for t in range(NT)])
    pop(L, 3)
    pop(L, N_C)
    pop(L, 5)

    h2T = push(L, "h2T", [128, 8, S], BF16, ["h2T"])
    wdn = push(L, "wdn", [128, NCP, 1024], BF16, ["wdn"])
    wu = [push(L, "wu%d" % i, [128, 8, 256], BF16, ["wu"]) for i in range(3)]

    def wu_load(s_):
        P.dma("pool", wu[s_ % 3][:], wup_d[s_ % NCP], writes=[("wu", s_ % 3)], slot=("wu", s_ % 3))
    NXE, NYE, NHE = 5, 2, 3
    xe = [push(R, "xe%d" % i, [128, D], F32, ["xe"]) for i in range(NXE)]
    ye = [push(R, "ye%d" % i, [128, D], F32, ["ye"]) for i in range(NYE)]
    h2b = [push(R, "h2b%d" % i, [128, D], BF16, ["h2b"]) for i in range(NHE)]
    N_E = NXE + NYE + NHE

    for pc in range(4):
        k0, k1 = [0, 6, 12, 17][pc], [6, 12, 17, 22][pc]
        P.dma("pool", wdn[:, k0:k1, :], wdn_d[:, k0:k1, :], writes=[("wdn", pc)], slot=("wdn", pc))
    WDN = [("wdn", pc) for pc in range(4)]
    wu_load(0)
    wu_load(1)

    def e1(t):
        yb = 2 * (t % 3)
        tsl = slice(t * 128, (t + 1) * 128)
        if t == 0:
            for t2 in range(2):
                P.dma("sp", xe[t2 % NXE][:], x_d[t2 * 128:(t2 + 1) * 128, :], writes=[("xe", t2 % NXE)],
                      slot=("xe", t2 % NXE))
        if t + 2 < NT:
            t2 = t + 2
            P.dma("sp", xe[t2 % NXE][:], x_d[t2 * 128:(t2 + 1) * 128, :], writes=[("xe", t2 % NXE)],
                  slot=("xe", t2 % NXE))
        for dh in range(2):
            o = psb(yb + dh)
            for c in range(8):
                src = attnT if c < 4 else lruT
                keys = [("attnT", t)] if c < 4 else [("lruT", c - 4)]
                mm(o, src[:, c % 4, tsl], wob[:, c, dh * 512:(dh + 1) * 512], c == 0, c == 7,
                   keys + [("wob", c)], psk(yb + dh))
        jt, jkey = jk()
        act(jt, psb(yb, 2), AF.Square, psk(yb, 2), [("st", "e_ss", t), jkey], accum=sc(ST2, 0, t))

    def e2(t):
        yb = 2 * (t % 3)
        tsl = slice(t * 128, (t + 1) * 128)
        xb, yy = xe[t % NXE], ye[t % NYE]
        XK = ("xe", t % NXE)
        act(sc(ST2, 1, t), sc(ST2, 0, t), AF.Sqrt, [("st", "e_ss", t)], [("st", "e_sd", t)], scale=1.0 / D, bias=EPS)
        recip(sc(ST2, 2, t), sc(ST2, 1, t), [("st", "e_sd", t)], [("st", "e_rs", t)])
        stt(yy[:], psb(yb, 2), sc(ST2, 2, t), g2[:, 1, :], ALU.mult, ALU.mult,
            psk(yb, 2) + [("st", "e_rs", t), ("g2", 1)], [("ye", t % NYE)])
        tt("dve", xb[:], xb[:], yy[:], ALU.add, [XK, ("ye", t % NYE)], [XK])
        P.dma("sp", out_d[tsl, :], xb[:], reads=[XK], writes=[("out", t)], slot=("x1o", t % NXE))
        if debug:
            if t == 0:
                dbgx1.append(nc.dram_tensor("dbg_x1", [S, D], F32, kind="ExternalOutput").ap())
            P.dma("sp", dbgx1[0][tsl, :], xb[:], reads=[XK], slot=("dbgx1", t % NXE))

    def e3(t):
        xb, hb2 = xe[t % NXE], h2b[t % NHE]
        XK = ("xe", t % NXE)
        jt, jkey = jk()
        act(jt, xb[:], AF.Square, [XK], [("st", "f_ss", t), jkey], accum=sc(ST2, 3, t))
        act(sc(ST2, 4, t), sc(ST2, 3, t), AF.Sqrt, [("st", "f_ss", t)], [("st", "f_sd", t)], scale=1.0 / D, bias=EPS)
        recip(sc(ST2, 5, t), sc(ST2, 4, t), [("st", "f_sd", t)], [("st", "f_rs", t)])
        stt(hb2[:], xb[:], sc(ST2, 5, t), g2[:, 0, :], ALU.mult, ALU.mult,
            [XK, ("st", "f_rs", t), ("g2", 0)], [("h2b", t % NHE)])

    def e4(t):
        hb2 = h2b[t % NHE]
        bank = 6 + t % 2
        pv = psb(bank).bitcast(BF16)
        for c in range(8):
            tr(pv[:, c * 128:(c + 1) * 128], hb2[:, c * 128:(c + 1) * 128], [("h2b", t % NHE), "ident"], psk(bank))
        src = pv.rearrange("p (c n) -> p c n", c=8)
        if t % 2:
            act(h2T[:, :, t * 128:(t + 1) * 128], src, AF.Copy, psk(bank), [("h2T", t)])
        else:
            cp("dve", h2T[:, :, t * 128:(t + 1) * 128], src, psk(bank), [("h2T", t)])

    dbgx1 = []
    drain(run_skewed([e1, e2, e3, e4], NT))
    P.dma("sp", g2[:, 1, :], gbc_d[:, 3, :], writes=[("g2", 1)], slot=("g2", 1))
    dump("h2T", h2T[:], [("h2T", t) for t in range(NT)])
    pop(R, N_E)
    pop(R, 5)

    uT = push(L, "uT", [128, NCP, 1024], BF16, ["uT"])
    tg = [push(L, "tg%d" % i, [128, 1024], F32, ["tgv"]) for i in range(2)]
    tv = [push(L, "tv%d" % i, [128, 1024], F32, ["tgv"]) for i in range(2)]
    gg = [push(L, "gg%d" % i, [128, 1024], F32, ["gg"]) for i in range(2)]
    xf = [push(R, "xf%d" % i, [128, D], F32, ["xf"]) for i in range(3)]
    yf = [push(R, "yf%d" % i, [128, D], F32, ["yf"]) for i in range(2)]

    def fw(ch, k):
        return pp[:, PP_FW + ch * 3 + k:PP_FW + ch * 3 + k + 1]

    def fb(ch):
        return pp[:, PP_FB + ch:PP_FB + ch + 1]

    step = 0

    for hf in range(2):
        for cpi in range(NCP):
            w = wu[step % 3]
            if step + 2 < 2 * NCP:
                wu_load(step + 2)
            base = 4 * (step % 2)
            for gv in range(2):
                for tbl in range(2):
                    b = base + 2 * gv + tbl
                    tb = 2 * hf + tbl
                    for c in range(8):
                        mm(psb(b), w[:, c, gv * 128:(gv + 1) * 128], h2T[:, c, tb * 512:(tb + 1) * 512],
                           c == 0, c == 7, [("h2T", 4 * tb + j) for j in range(4)] + [("wu", step % 3)], psk(b))
            bufs = (tg[step % 2], tv[step % 2])
            for gv in range(2):
                ch = cpi + NCP * gv
                G = psb(base + 2 * gv, 2)
                T = bufs[gv]
                kp = psk(base + 2 * gv, 2)
                tk = ("tgv", gv, step % 2)
                act(T[:], G, AF.Identity, kp + ["pp"], [tk], scale=fw(ch, 2), bias=fb(ch))
                if hf == 0:
                    act(fprev[:, ch, :], G[:, 1022:1024], AF.Copy, kp, [("fprev", ch)])
                stt(T[:, 1:1024], G[:, 0:1023], fw(ch, 1), T[:, 1:1024], ALU.mult, ALU.add, kp + [tk, "pp"], [tk])
                stt(T[:, 2:1024], G[:, 0:1022], fw(ch, 0), T[:, 2:1024], ALU.mult, ALU.add, kp + [tk, "pp"], [tk])
                if hf == 1:
                    stt(T[:, 0:2], fprev[:, ch, 0:2], fw(ch, 0), T[:, 0:2], ALU.mult, ALU.add,
                        [("fprev", ch), tk, "pp"], [tk])
                    stt(T[:, 0:1], fprev[:, ch, 1:2], fw(ch, 1), T[:, 0:1], ALU.mult, ALU.add,
                        [("fprev", ch), tk, "pp"], [tk])
            g_ = gg[step % 2]
            act(g_[:], bufs[0][:], AF.Gelu_apprx_tanh, [("tgv", 0, step % 2)], [("gg", step % 2)])
            tt("pool", uT[:, cpi, :], g_[:], bufs[1][:], ALU.mult, [("gg", step % 2), ("tgv", 1, step % 2)],
               [("uT", cpi)])
            step += 1
        for tl in range(8):
            t = hf * 8 + tl
            tsl = slice(t * 128, (t + 1) * 128)
            yb = 2 * (t % 4)
            xb, yy = xf[t % 3], yf[t % 2]
            XK = ("xf", t % 3)
            P.dma("sp", xb[:], out_d[tsl, :], reads=[("out", t)], writes=[XK], slot=("xf", t % 3))
            for dh in range(2):
                o = psb(yb + dh)
                for k in range(NCP):
                    mm(o, uT[:, k, tl * 128:(tl + 1) * 128], wdn[:, k, dh * 512:(dh + 1) * 512],
                       k == 0, k == NCP - 1, [("uT", k)] + WDN, psk(yb + dh))
            Y = psb(yb, 2)
            jt, jkey = jk()
            act(jt, Y, AF.Square, psk(yb, 2), [("st", "g_ss", t), jkey], accum=sc(ST3, 0, t))
            act(sc(ST3, 1, t), sc(ST3, 0, t), AF.Sqrt, [("st", "g_ss", t)], [("st", "g_sd", t)], scale=1.0 / D, bias=EPS)
            recip(sc(ST3, 2, t), sc(ST3, 1, t), [("st", "g_sd", t)], [("st", "g_rs", t)])
            stt(yy[:], Y, sc(ST3, 2, t), g2[:, 1, :], ALU.mult, ALU.mult,
                psk(yb, 2) + [("st", "g_rs", t), ("g2", 1)], [("yf", t % 2)])
            tt("dve", xb[:], xb[:], yy[:], ALU.add, [XK, ("yf", t % 2)], [XK])
            P.dma("sp", out_d[tsl, :], xb[:], reads=[XK], writes=[("out", t)], slot=("xoo", t % 3))

    dump("uT", uT[:], [("uT", k) for k in range(NCP)])
    P.emit()
    psc.__exit__(None, None, None)
    return nc


def _host_layout(inp):
    f = np.float32
    w_in = np.asarray(inp["w_in"][0], f)
    cols = []
    for fc in range(4):
        cols.append(np.arange(fc * 128, (fc + 1) * 128))
    for h in range(2):
        r = 512 + h * 64 + np.arange(64)
        cols.append(np.concatenate([r, r]))
    cols.append(640 + np.arange(128))
    cols.append(640 + np.arange(128))
    for c in range(4):
        cols.append(768 + c * 128 + np.arange(128))
    for c in range(4):
        cols.append(1280 + c * 128 + np.arange(128))
    w_in_r = np.stack([w_in[:, cc].reshape(8, 128, 128).transpose(1, 0, 2) for cc in cols]).astype(f)
    w_out_r = np.ascontiguousarray(np.asarray(inp["w_out"][0], f).reshape(8, 128, 1024).transpose(1, 0, 2))
    w_up = np.asarray(inp["w_up"][0], f)
    w_up_r = np.empty((NCP, 128, 8, 256), f)
    for cpi in range(NCP):
        g = w_up[:, cpi * 128:(cpi + 1) * 128].reshape(8, 128, 128).transpose(1, 0, 2)
        v = w_up[:, DFF + cpi * 128:DFF + (cpi + 1) * 128].reshape(8, 128, 128).transpose(1, 0, 2)
        w_up_r[cpi, :, :, 0:128] = g
        w_up_r[cpi, :, :, 128:256] = v
    w_down_r = np.ascontiguousarray(np.asarray(inp["w_down"][0], f).reshape(NCP, 128, 1024).transpose(1, 0, 2))

    def pc(v, n):
        return np.asarray(v, f).reshape(n, 128).T

    pp = np.zeros((128, NPP), f)
    cwt = np.asarray(inp["lru_conv_w"][0], f)
    for c in range(4):
        for k in range(4):
            pp[:, c * 4 + k] = cwt[k, c * 128:(c + 1) * 128]
    pp[:, 16:20] = pc(inp["lru_conv_b"][0], 4)
    pp[:, 20:24] = pc(inp["lru_ba"][0], 4)
    pp[:, 24:28] = pc(inp["lru_bx"][0], 4)
    pp[:, 28:32] = pc(inp["lru_lambda"][0], 4)
    fwt = np.asarray(inp["ffn_conv_w"][0], f)
    for ch in range(44):
        for k in range(3):
            pp[:, 32 + ch * 3 + k] = fwt[k, ch * 128:(ch + 1) * 128]
    pp[:, 164:208] = pc(inp["ffn_conv_b"][0], 44)
    pp[:, 208:212] = pc(inp["norm_attn_out"][0], 4)
    pp[:, 212:216] = pc(inp["norm_lru_out"][0], 4)

    gbc = np.empty((128, 4, 1024), f)
    for i, k in enumerate(["norm_mix_pre", "norm_mix_post", "norm_ffn_pre", "norm_ffn_post"]):
        gbc[:, i, :] = np.asarray(inp[k][0], f)[None, :]

    btab = np.empty((128, 2, 2, 4, 128), f)
    kk = np.arange(128)[:, None]
    qq = np.arange(128)[None, :]
    for h in range(2):
        for rt in range(2):
            for slot in range(4):
                j, prev = slot % 2, slot // 2
                hq = 4 * h + 2 * j + rt
                slope = 2.0 ** (-(hq + 1))
                dist = (qq - kk) + (128 if prev else 0)
                valid = (dist >= 0) & (dist < 128)
                btab[:, h, rt, slot, :] = np.where(valid, -slope * dist, -30000.0)
    sinks = np.asarray(inp["sinks"][0], f)
    sinkbc = np.ascontiguousarray(np.broadcast_to(sinks[None, :], (128, 8))).astype(f)

    def bd(w):
        w = np.asarray(w[0], f)
        o = np.zeros((128, 4, 128), f)
        for c in range(4):
            for hh in range(2):
                o[hh * 64:(hh + 1) * 64, c, hh * 64:(hh + 1) * 64] = w[2 * c + hh]
        return o

    shared = {
        "w_in_r": w_in_r, "w_out_r": w_out_r, "w_up_r": w_up_r, "w_down_r": w_down_r,
        "pp": pp, "gbc": gbc, "btab": btab.reshape(128, 2048), "sinkbc": sinkbc,
        "wab": bd(inp["lru_wa"]), "wxb": bd(inp["lru_wx"]), "ident": np.eye(128, dtype=f),
    }
    return shared


_NC_CACHE = {}


def kernel(**inputs):
    x = np.asarray(inputs["x"], np.float32)
    shared = _host_layout(inputs)
    if "nc" not in _NC_CACHE:
        _NC_CACHE["nc"] = build_nc()
    nc = _NC_CACHE["nc"]
    in_maps = []
    for b in range(8):
        m = dict(shared)
        m["x"] = np.ascontiguousarray(x[b])
        in_maps.append(m)
    res = run_bass_kernel_spmd(nc, in_maps, core_ids=list(range(8)))
    return np.stack([np.asarray(r["out"], np.float32) for r in res.results], axis=0)
```
